# Optimizing a Trainium2 kernel written in Bass

```python
import math
import jax, jax.numpy as jnp
from jax import lax
import numpy as np

D_MODEL = 1024
BATCH = 4
SEQ = 4096
DEPTH = 1

MIX_WIDTH = D_MODEL
DIFF_HEADS = 4
DIFF_HEAD_DIM = 64
DIFF_V_DIM = 2 * DIFF_HEAD_DIM
RET_HEADS = 4
RET_QK_DIM = 64
RET_V_DIM = 2 * RET_QK_DIM
DIFF_Q_W = DIFF_HEADS * 2 * DIFF_HEAD_DIM
DIFF_V_W = DIFF_HEADS * DIFF_V_DIM
RET_QK_W = RET_HEADS * RET_QK_DIM
RET_V_W = RET_HEADS * RET_V_DIM
IN_SPLITS = (DIFF_Q_W, DIFF_Q_W, DIFF_V_W, RET_QK_W, RET_QK_W, RET_V_W, RET_V_W, D_MODEL, D_MODEL)
IN_COLS = sum(IN_SPLITS)
D_FF = ((8 * D_MODEL // 3 + 255) // 256) * 256
ROPE_THETA = 10000.0
Q_BLOCK = 128
RET_CHUNK = 128
NORM_EPS = 1e-5

kernel_name = "hybrid_diffattn_retention_gated_block"


def rmsnorm(x, g=None, eps=NORM_EPS):
    x32 = x.astype(jnp.float32)
    y = x32 * lax.rsqrt(jnp.mean(x32 * x32, axis=-1, keepdims=True) + eps)
    if g is not None:
        y = y * g.astype(jnp.float32)
    return y.astype(x.dtype)


def rope_tables(seq_len, dim):
    inv_freq = ROPE_THETA ** (-jnp.arange(0, dim, 2, dtype=jnp.float32) / dim)
    ang = jnp.arange(seq_len, dtype=jnp.float32)[:, None] * inv_freq[None, :]
    return jnp.cos(ang), jnp.sin(ang)


def apply_rope(x, cos, sin):
    half = x.shape[-1] // 2
    shp = (x.shape[1],) + (1,) * (x.ndim - 3) + (half,)
    c = cos.reshape(shp).astype(x.dtype)
    s = sin.reshape(shp).astype(x.dtype)
    x1, x2 = x[..., :half], x[..., half:]
    return jnp.concatenate([x1 * c - x2 * s, x2 * c + x1 * s], axis=-1)


def differential_attention(q, k, v, lam):
    b, s, h, _, d = q.shape
    nqb = s // Q_BLOCK
    scale = d ** -0.5
    qb = q.reshape(b, nqb, Q_BLOCK, h, 2, d).transpose(1, 0, 2, 3, 4, 5)

    def block(q_blk):
        sc = jnp.einsum('bqhcd,bkhcd->bhcqk', q_blk, k).astype(jnp.float32) * scale
        p = jax.nn.softmax(sc, axis=-1)
        a = (p[:, :, 0] - lam * p[:, :, 1]).astype(v.dtype)
        return jnp.einsum('bhqk,bkhe->bqhe', a, v)

    out = lax.map(block, qb)
    return out.transpose(1, 0, 2, 3, 4).reshape(b, s, h, v.shape[-1])


def retention_one_direction(q, k, v, log_gamma, include_diag):
    b, s, h, dk = q.shape
    dv = v.shape[-1]
    c = RET_CHUNK
    n = s // c
    dt = q.dtype
    idx = jnp.arange(c, dtype=jnp.float32)
    diff = idx[:, None] - idx[None, :]
    mask = diff >= 0 if include_diag else diff > 0
    dmat = jnp.where(mask[None], jnp.exp(jnp.where(mask, diff, 0.0)[None] * log_gamma[:, None, None]), 0.0).astype(dt)
    zeta = jnp.exp((c - 1 - idx)[:, None] * log_gamma[None, :]).astype(dt)
    xi = jnp.exp((idx + 1)[:, None] * log_gamma[None, :]).astype(dt)
    g_chunk = jnp.exp(c * log_gamma).astype(dt)

    def to_chunks(t):
        return t.reshape(b, n, c, h, t.shape[-1]).transpose(1, 0, 2, 3, 4)

    def step(state, inp):
        qc, kc, vc = inp
        sc = jnp.einsum('bihd,bjhd->bhij', qc, kc) * dmat[None]
        inner = jnp.einsum('bhij,bjhe->bihe', sc, vc)
        cross = jnp.einsum('bihd,bhde->bihe', qc, state) * xi[None, :, :, None]
        new_state = g_chunk[None, :, None, None] * state + jnp.einsum(
            'bjhd,bjhe->bhde', kc * zeta[None, :, :, None], vc)
        return new_state, inner + cross

    state0 = jnp.zeros((b, h, dk, dv), dt)
    _, ys = lax.scan(step, state0, (to_chunks(q), to_chunks(k), to_chunks(v)))
    return ys.transpose(1, 0, 2, 3, 4).reshape(b, s, h, dv)


def bidirectional_retention(q, k, v, logit_fwd, logit_bwd):
    lg_f = jax.nn.log_sigmoid(logit_fwd.astype(jnp.float32))
    lg_b = jax.nn.log_sigmoid(logit_bwd.astype(jnp.float32))
    fwd = retention_one_direction(q, k, v, lg_f, True)
    flip = lambda t: jnp.flip(t, axis=1)
    bwd = flip(retention_one_direction(flip(q), flip(k), flip(v), lg_b, False))
    return fwd + bwd


def setup_inputs(seed: int = 0) -> dict:
    key = jax.random.key(seed)
    ks = jax.random.split(key, 20)
    f32 = jnp.float32

    def nrm(k, shape, fan_in):
        return jax.random.normal(k, shape, f32) * (fan_in ** -0.5)

    def gain(k, shape):
        return 1.0 + 0.02 * jax.random.normal(k, shape, f32)

    p = jnp.exp2(-5.0 - jnp.arange(RET_HEADS, dtype=f32))
    base_logit = jnp.log1p(-p) - jnp.log(p)
    return {
        "x": jax.random.normal(ks[0], (BATCH, SEQ, D_MODEL), f32),
        "g_mix": gain(ks[1], (DEPTH, D_MODEL)),
        "w_in": nrm(ks[2], (DEPTH, D_MODEL, IN_COLS), D_MODEL),
        "diff_lq1": 0.1 * jax.random.normal(ks[3], (DEPTH, DIFF_HEAD_DIM), f32),
        "diff_lk1": 0.1 * jax.random.normal(ks[4], (DEPTH, DIFF_HEAD_DIM), f32),
        "diff_lq2": 0.1 * jax.random.normal(ks[5], (DEPTH, DIFF_HEAD_DIM), f32),
        "diff_lk2": 0.1 * jax.random.normal(ks[6], (DEPTH, DIFF_HEAD_DIM), f32),
        "diff_subln_g": gain(ks[7], (DEPTH, DIFF_V_DIM)),
        "ret_decay_fwd": base_logit[None] + 0.1 * jax.random.normal(ks[8], (DEPTH, RET_HEADS), f32),
        "ret_decay_bwd": base_logit[None] + 0.1 * jax.random.normal(ks[9], (DEPTH, RET_HEADS), f32),
        "w_up_diff": nrm(ks[10], (DEPTH, DIFF_V_W, D_MODEL), DIFF_V_W),
        "w_up_ret": nrm(ks[11], (DEPTH, RET_V_W, D_MODEL), RET_V_W),
        "w_o": nrm(ks[12], (DEPTH, D_MODEL, D_MODEL), D_MODEL),
        "g_ffn": gain(ks[13], (DEPTH, D_MODEL)),
        "w_ffn_gate": nrm(ks[14], (DEPTH, D_MODEL, D_FF), D_MODEL),
        "w_ffn_up": nrm(ks[15], (DEPTH, D_MODEL, D_FF), D_MODEL),
        "w_ffn_down": nrm(ks[16], (DEPTH, D_FF, D_MODEL), D_FF),
        "g_final": gain(ks[17], (D_MODEL,)),
    }


def reference(x, g_mix, w_in, diff_lq1, diff_lk1, diff_lq2, diff_lk2, diff_subln_g,
              ret_decay_fwd, ret_decay_bwd, w_up_diff, w_up_ret, w_o,
              g_ffn, w_ffn_gate, w_ffn_up, w_ffn_down, g_final):
    b, s, _ = x.shape
    cos, sin = rope_tables(s, DIFF_HEAD_DIM)
    split_pts = list(np.cumsum(IN_SPLITS)[:-1])
    for l in range(DEPTH):
        lam_init = 0.8 - 0.6 * math.exp(-0.3 * l)
        h = rmsnorm(x, g_mix[l])
        proj = h @ w_in[l]
        dq, dk, dv, rq, rk, rv, rg, gate_a, gate_b = jnp.split(proj, split_pts, axis=-1)

        dq = apply_rope(dq.reshape(b, s, DIFF_HEADS, 2, DIFF_HEAD_DIM), cos, sin)
        dk = apply_rope(dk.reshape(b, s, DIFF_HEADS, 2, DIFF_HEAD_DIM), cos, sin)
        dv = dv.reshape(b, s, DIFF_HEADS, DIFF_V_DIM)
        lam = (jnp.exp(jnp.sum(diff_lq1[l].astype(jnp.float32) * diff_lk1[l].astype(jnp.float32)))
               - jnp.exp(jnp.sum(diff_lq2[l].astype(jnp.float32) * diff_lk2[l].astype(jnp.float32)))
               + lam_init)
        a = differential_attention(dq, dk, dv, lam)
        a = rmsnorm(a, diff_subln_g[l]) * (1.0 - lam_init)
        ya = a.reshape(b, s, DIFF_V_W) @ w_up_diff[l]

        rq = apply_rope(rq.reshape(b, s, RET_HEADS, RET_QK_DIM), cos, sin)
        rk = apply_rope(rk.reshape(b, s, RET_HEADS, RET_QK_DIM), cos, sin) * (RET_QK_DIM ** -0.5)
        rv = rv.reshape(b, s, RET_HEADS, RET_V_DIM)
        r = bidirectional_retention(rq, rk, rv, ret_decay_fwd[l], ret_decay_bwd[l])
        r = rmsnorm(r).reshape(b, s, RET_V_W) * jax.nn.silu(rg)
        yb = r @ w_up_ret[l]

        m = jax.nn.sigmoid(gate_a) * ya + jax.nn.sigmoid(gate_b) * yb
        x = x + m @ w_o[l]

        h2 = rmsnorm(x, g_ffn[l])
        x = x + (jax.nn.silu(h2 @ w_ffn_gate[l]) * (h2 @ w_ffn_up[l])) @ w_ffn_down[l]
    return rmsnorm(x, g_final)
```

```python
import math
from contextlib import ExitStack

import numpy as np
import concourse.bass as bass
import concourse.mybir as mybir
from concourse.bass_utils import run_bass_kernel_spmd

F32 = mybir.dt.float32
BF16 = mybir.dt.bfloat16
AF = mybir.ActivationFunctionType
ALU = mybir.AluOpType

P = 128
NOWN = 16
EPS = 1e-5
LN8 = math.log(0.125)
SAME_ENGINE_SYNC = True
SERIALIZE = True
import os
ROPE_ADD_ENG = os.environ.get("ROPE_ADD_ENG", "dve")
EARLY = ("setup", "normA", "lw", "kp", "kvA", "other", "ret", "own1", "att")


class _Op:
    __slots__ = ("eng", "fn", "dma", "deps", "waits", "inc", "val", "sem", "idx", "wkeys", "phase")


class KB:
    COMPUTE = ("pe", "act", "dve", "pool")

    def __init__(self, nc):
        self.nc = nc
        self.ops = []
        self.last_w = {}
        self.readers = {}
        self.phase = 0
        self._last_barrier = 0

    def add(self, eng, fn, reads=(), writes=(), dma=False, extra_deps=()):
        op = _Op()
        op.eng, op.fn, op.dma = eng, fn, dma
        op.idx = len(self.ops)
        op.phase = self.phase
        deps = set(extra_deps)
        if SERIALIZE and op.idx > 0:
            deps.add(op.idx - 1)
        for k in reads:
            w = self.last_w.get(k)
            if w is not None:
                deps.add(w)
        for k in writes:
            w = self.last_w.get(k)
            if w is not None:
                deps.add(w)
            deps.update(self.readers.get(k, ()))
        for k in reads:
            self.readers.setdefault(k, []).append(op.idx)
        for k in writes:
            self.last_w[k] = op.idx
            self.readers[k] = []
        deps.discard(op.idx)
        op.deps = deps
        op.waits = []
        op.wkeys = []
        op.inc = False
        op.val = 0
        op.sem = writes[0] if dma else None
        self.ops.append(op)
        return op.idx

    def barrier(self):
        last = {}
        for op in self.ops[self._last_barrier:]:
            last[op.eng] = op.idx
        deps = list(last.values())
        deps += [op.idx for op in self.ops[self._last_barrier:] if op.dma]
        for e in ("pe", "act", "dve", "pool", "sp"):
            self.add(e, lambda en: en.nop(), extra_deps=deps)
        self._last_barrier = len(self.ops)
        self.phase += 1

    def _eff_deps(self, op):
        ops = self.ops
        best = {}
        out = []
        for d in op.deps:
            dop = ops[d]
            if dop.dma:
                out.append(dop)
                continue
            if dop.eng == "sp":
                continue
            if dop.phase != op.phase:
                continue
            if dop.eng == op.eng and (dop.eng == "pe" or not SAME_ENGINE_SYNC) and not op.dma:
                continue
            if dop.eng not in best or best[dop.eng].idx < dop.idx:
                best[dop.eng] = dop
        return out + list(best.values())

    def emit(self, es):
        nc = self.nc
        ops = self.ops
        eff = [self._eff_deps(op) for op in ops]
        for lst in eff:
            for dop in lst:
                if not dop.dma:
                    dop.inc = True
        esem = {}
        dsem = {}
        cnt = {}
        dcnt = {}
        for op in ops:
            if op.dma:
                k = op.sem
                if k not in dsem:
                    dsem[k] = es.enter_context(nc.semaphore("d%d" % len(dsem)))
                    dcnt[k] = 0
                dcnt[k] += 16
                op.val = dcnt[k]
            elif op.inc:
                k = (op.eng, op.phase)
                if k not in esem:
                    esem[k] = es.enter_context(nc.semaphore("s_%s%d" % k))
                    cnt[k] = 0
                cnt[k] += 1
                op.val = cnt[k]
        seen = {}
        for op, lst in zip(ops, eff):
            need = {}
            for dop in lst:
                if dop.dma:
                    s = dsem[dop.sem]
                    key = ("d", dop.sem)
                else:
                    s = esem[(dop.eng, dop.phase)]
                    key = ("e", dop.eng, dop.phase)
                if dop.val > need.get(key, (None, 0))[1]:
                    need[key] = (s, dop.val)
            sm = seen.setdefault(op.eng, {})
            for key, (s, v) in need.items():
                if sm.get(key, 0) >= v:
                    continue
                sm[key] = v
                op.waits.append((s, v))
                op.wkeys.append((key, v))
        by = {}
        for op in ops:
            by.setdefault(op.eng, []).append(op)
        semv = {}
        ptr = {e: 0 for e in by}
        progress = True
        while progress:
            progress = False
            for e, lst in by.items():
                while ptr[e] < len(lst):
                    op = lst[ptr[e]]
                    if all(semv.get(k, 0) >= v for (k, v) in op.wkeys):
                        if op.dma:
                            semv[("d", op.sem)] = semv.get(("d", op.sem), 0) + 16
                        elif op.inc:
                            k = ("e", op.eng, op.phase)
                            semv[k] = semv.get(k, 0) + 1
                        ptr[e] += 1
                        progress = True
                    else:
                        break
        stuck = {e: (ptr[e], len(lst)) for e, lst in by.items() if ptr[e] < len(lst)}
        if stuck:
            msg = []
            for e in stuck:
                op = by[e][ptr[e]]
                msg.append((e, op.idx, op.wkeys, {k: semv.get(k, 0) for k, _ in op.wkeys}))
            raise RuntimeError("DEADLOCK in wait graph: %r" % (msg,))
        print("kb: ops", {e: len(l) for e, l in by.items()}, "sems", len(esem) + len(dsem),
              "incs", {e: sum(1 for o in l if o.inc) for e, l in by.items()}, "maxval", max(list(cnt.values()) + [0]))
        block = es.enter_context(nc.Block())

        def run(engine, lst):
            for op in lst:
                for (s, v) in op.waits:
                    engine.wait_ge(s, v)
                ins = op.fn(engine)
                if op.dma:
                    ins.then_inc(dsem[op.sem], 16)
                elif op.inc:
                    ins.then_inc(esem[(op.eng, op.phase)], 1)

        @block.tensor
        def _(e):
            run(e, by.get("pe", []))

        @block.scalar
        def _(e):
            run(e, by.get("act", []))

        @block.vector
        def _(e):
            run(e, by.get("dve", []))

        @block.gpsimd
        def _(e):
            run(e, by.get("pool", []))

        @block.sync
        def _(e):
            run(e, by.get("sp", []))


def build(stop_after=None):
    nc = bass.Bass("TRN2", target_bir_lowering=False)

    def din(name, shape):
        return nc.dram_tensor(name, list(shape), F32, kind="ExternalInput").ap()

    xa = din("xa", [4096, 1024])
    tabd = din("tab", [4096, 96])
    cst = din("cst", [128, 424])
    w_in = din("w_in", [1024, 5120])
    w_upa = din("w_up_diff", [512, 1024])
    w_upb = din("w_up_ret", [512, 1024])
    w_o = din("w_o", [1024, 1024])
    small = stop_after in EARLY
    w_fg = din("w_ffn_gate", [1024, 2816] if not small else [1, 1])
    w_fu = din("w_ffn_up", [1024, 2816] if not small else [1, 1])
    w_fd = din("w_ffn_down", [2816, 1024] if not small else [1, 1])
    g_mix = din("g_mix", [128, 8])
    g_ffn = din("g_ffn", [128, 8])
    g_fin = din("g_final", [1024])
    g_sub = din("diff_subln_g", [128, 1])
    lqk = din("lqk", [256])
    dec = din("dec", [8])
    y = nc.dram_tensor("y", [2048, 1024], F32, kind="ExternalOutput").ap()
    dbg_out = {}

    kb = KB(nc)
    with ExitStack() as es:
        def sb(name, shape, dt):
            return es.enter_context(nc.sbuf_tensor(name, list(shape), dt))

        ps = es.enter_context(nc.psum_tensor("ps", [128, 8, 512], F32))

        cs = sb("csb", [128, 424], F32)
        identb = sb("identb", [128, 128], BF16)
        zerob = sb("zerob", [128, 512], BF16)
        tab = sb("tabs", [128, 16 * 96], F32)
        tabv = tab[:, :].rearrange("p (t c) -> p t c", c=96)
        sm = sb("sm", [128, 256], F32)
        rett = sb("rett", [128, 512], F32)
        stS = sb("stS", [128, 4 * 256], F32)
        xt = [sb("xt%d" % i, [128, 1024], F32) for i in range(2)]
        xn = [sb("xn%d" % i, [128, 1024], BF16) for i in range(2)]
        junk = sb("junk", [128, 1024], BF16)
        tFall = sb("tFall", [128, 2048], F32)
        tF = [tFall[:, i * 512:(i + 1) * 512] for i in range(4)]
        gfb = tFall[:, 1024:2048]
        tB = [sb("tB%d" % i, [128, 1024], BF16) for i in range(3)]
        stg = [sb("stg%d" % i, [128, 2048], F32) for i in range(2)]
        Rw = [sb("Rw%d" % i, [128, 4096], BF16) for i in range(2)]
        RhT = sb("RhT", [128, 16384], BF16)
        RQT = sb("RQT", [128, 8192], BF16)
        RKT = sb("RKT", [128, 16384], BF16)
        RV1 = sb("RV1", [128, 32 * 516], BF16)
        RrnT = sb("RrnT", [128, 8192], BF16)

        hT = RhT[:, :].rearrange("p (c t) -> p c t", c=8)
        QT = RQT[:, :].rearrange("p (h t) -> p h t", h=4)
        KT = RKT[:, :].rearrange("p (h t) -> p h t", h=4)
        V1 = RV1[:, :].rearrange("p (t h e) -> p t h e", t=32, h=4)
        rnT = RrnT[:, :].rearrange("p (h t) -> p h t", h=4)
        rvs = RKT[:, :].rearrange("p (h t) -> p h t", h=4)
        Sfb = RV1[:, 0:4096].rearrange("p (t c) -> p t c", c=256)
        Ubs = RV1[:, 4096:8192].rearrange("p (t c) -> p t c", c=256)
        mtok = RKT[:, :].rearrange("p (t c) -> p t c", c=1024)
        mT = RV1[:, 0:16384].rearrange("p (c t) -> p c t", c=8)
        x1a = RhT[:, :].bitcast(F32).rearrange("p (t c) -> p t c", c=1024)
        x1b = RKT[:, :].bitcast(F32).rearrange("p (t c) -> p t c", c=1024)

        def x1(ti):
            return x1a[:, ti, :] if ti < 8 else x1b[:, ti - 8, :]

        def x1key(ti):
            return ("x1", ti)

        ident_f = cs[:, 0:128]
        posm = cs[:, 128:256]
        negm = cs[:, 256:384]
        jvec = cs[:, 384:388]
        idxA = cs[:, 388:404]
        idxB = cs[:, 404:420]
        flags = cs[:, 420:422]
        c_eps = sm[:, 0:1]
        c_ln8 = sm[:, 1:2]
        lam = sm[:, 2:3]
        neglam = sm[:, 3:4]
        gsub = sm[:, 4:5]
        s12 = sm[:, 5:7]
        e12 = sm[:, 7:9]
        gmix = sm[:, 16:24]
        gffn = sm[:, 24:32]
        decb = sm[:, 32:40]
        lg = sm[:, 40:48]
        zf = sm[:, 48:52]
        zb = sm[:, 52:56]
        xif = sm[:, 56:60]
        xib = sm[:, 60:64]
        lgsf = sm[:, 64:66]
        lgsb = sm[:, 66:68]
        gfc = sm[:, 68:70]
        gbc = sm[:, 70:72]
        gfpow = sm[:, 72:104].rearrange("p (a b) -> p a b", a=2)
        gbpow = sm[:, 104:136].rearrange("p (a b) -> p a b", a=2)
        nrm = sm[:, 136:200].rearrange("p (a b) -> p a b", b=16)
        DT = rett[:, 0:512].rearrange("p (a b) -> p a b", a=2)
        dtmp = tFall[:, 0:512]
        lqs = tFall[:, 512:768]
        Sf = stS[:, 0:256]
        Sbp = [stS[:, 256:512], stS[:, 512:768]]

        state = {"stg": 0, "bank": 0, "nrm": 0}

        def mm(out, lhsT, rhs, start, stop, reads, writes, **kw):
            kb.add("pe", lambda e: e.matmul(out, lhsT, rhs, start=start, stop=stop, **kw), reads, writes)

        def tr(out, in_, reads, writes):
            kb.add("pe", lambda e: e.transpose(out, in_, identb[:, :]), list(reads) + ["identb"], writes)

        def act(out, in_, func, reads, writes, **kw):
            kb.add("act", lambda e: e.activation(out, in_, func, **kw), reads, writes)

        def ve(eng, name, reads, writes, *a, **kw):
            kb.add(eng, lambda e: getattr(e, name)(*a, **kw), reads, writes)

        def dma(out, in_, reads, writes, q="sp"):
            kb.add(q, lambda e: e.dma_start(out=out, in_=in_), reads, writes, dma=True)

        def psb(b):
            return ps[:, b, :]

        def psbf(b):
            return ps[:, b, :].bitcast(BF16)

        def nslot():
            state["nrm"] = (state["nrm"] + 1) % 4
            return state["nrm"]

        def load_w(dst, dkey, src, nch, ncols, fold):
            per = 2048 // nch
            c0 = 0
            part = 0
            while c0 < ncols:
                w = min(per, ncols - c0)
                s = state["stg"]
                state["stg"] = 1 - s
                sv = stg[s][:, 0:nch * w].rearrange("p (c n) -> p c n", c=nch)
                dma(sv, src[:, c0:c0 + w].rearrange("(c p) n -> p c n", p=128), [], [("stg", s)])
                if fold is None:
                    act(dst[:, :, c0:c0 + w], sv, AF.Identity, [("stg", s)], [dkey])
                else:
                    for c in range(nch):
                        fa = fold[:, c:c + 1] if fold.shape[1] > 1 else fold[:, 0:1]
                        act(dst[:, c, c0:c0 + w], sv[:, c, :], AF.Identity, [("stg", s), "consts"], [dkey], scale=fa)
                c0 += w
                part += 1

        def wkeys(dkey, nch, ncols, folded):
            return [dkey]

        def wslot(i, nch=8):
            return Rw[i][:, :].rearrange("p (c n) -> p c n", c=nch)

        def norm_tile(src_ap, srckey_reads, s, dst, dstkey, bank):
            k = nslot()
            ss = nrm[:, k, 0:1]
            rt = nrm[:, k, 1:2]
            rs = nrm[:, k, 2:3]
            act(junk[:, :], src_ap, AF.Square, srckey_reads, ["junk", ("nrm", k)], accum_out=ss)
            act(rt, ss, AF.Sqrt, [("nrm", k), "consts"], [("nrm", k, 1)], scale=1.0 / 1024, bias=c_eps)
            ve("dve", "reciprocal", [("nrm", k, 1)], [("nrm", k, 2)], rs, rt)
            act(xn[s][:, :], src_ap, AF.Identity, list(srckey_reads) + [("nrm", k, 2)], [("xn", s)], scale=rs)
            pb = psbf(bank)
            for c in range(8):
                tr(pb[:, c * 128:(c + 1) * 128], xn[s][:, c * 128:(c + 1) * 128], [("xn", s)], [("ps", bank)])
            ve("dve", "tensor_copy", [("ps", bank)], [dstkey], dst, pb.rearrange("p (c t) -> p c t", c=8))

        def rope(bank, ncols, tt, out_bf, outkey, tslot):
            G = ncols // 64
            X = ps[:, bank, 0:ncols].rearrange("p (g t d) -> p g t d", g=G, t=2)
            ia, ib = 2 * tslot, 2 * tslot + 1
            A = tF[ia][:, 0:ncols].rearrange("p (g t d) -> p g t d", g=G, t=2)
            B = tF[ib][:, 0:ncols].rearrange("p (g t d) -> p g t d", g=G, t=2)
            cosb = tabv[:, tt, 0:32].unsqueeze(1).unsqueeze(1).broadcast_to([128, G, 2, 32])
            sinb = tabv[:, tt, 32:64].unsqueeze(1).broadcast_to([128, G, 32])
            nsinb = tabv[:, tt, 64:96].unsqueeze(1).broadcast_to([128, G, 32])
            ka, kbk = ("tF", ia), ("tF", ib)
            act(tF[ia][:, 0:ncols], ps[:, bank, 0:ncols], AF.Identity, [("ps", bank)], [ka])
            ve("dve", "tensor_tensor", [ka, "tab"], [(kbk, 0)], B[:, :, 0, :], A[:, :, 1, :], nsinb, ALU.mult)
            ve("dve", "tensor_tensor", [ka, "tab"], [(kbk, 1)], B[:, :, 1, :], A[:, :, 0, :], sinb, ALU.mult)
            ve("dve", "tensor_tensor", [ka, "tab", (kbk, 0), (kbk, 1)], [ka], A, A, cosb, ALU.mult)
            ve(ROPE_ADD_ENG, "tensor_tensor", [ka, (kbk, 0), (kbk, 1)], [outkey], out_bf,
               tF[ia][:, 0:ncols], tF[ib][:, 0:ncols], ALU.add)

        def proj_tok(bank, ncols, ti, wv, wk):
            for c in range(8):
                mm(ps[:, bank, 0:ncols], hT[:, c, ti * 128:(ti + 1) * 128], wv[:, c, 0:ncols],
                   c == 0, c == 7, [("hT", ti)] + wk, [("ps", bank)])

        ve("dve", "memset", [], ["sm0"], sm[:, :], 0.0)
        kb.barrier()
        dma(cs[:, :], cst, [], ["cs"])
        dma(gmix, g_mix, [], ["gm"])
        dma(gffn, g_ffn, [], ["gf"])
        dma(gsub, g_sub, [], ["gs"])
        dma(decb, dec.partition_broadcast(128), [], ["decb"])
        dma(lqs, lqk.partition_broadcast(128), [], ["lq"])
        ve("dve", "memset", [], ["c0"], c_eps, EPS)
        ve("dve", "memset", [], ["c1"], c_ln8, LN8)
        ve("dve", "memset", [], ["zerob"], zerob[:, :], 0.0)
        ve("dve", "tensor_copy", ["cs"], ["identb"], identb[:, :], ident_f)
        ve("dve", "tensor_tensor", ["lq"], ["p1"], dtmp[:, 0:64], lqs[:, 0:64], lqs[:, 64:128], ALU.mult)
        ve("dve", "tensor_tensor", ["lq"], ["p2"], dtmp[:, 64:128], lqs[:, 128:192], lqs[:, 192:256], ALU.mult)
        act(dtmp[:, 128:192], dtmp[:, 0:64], AF.Identity, ["p1"], ["s1"], accum_out=s12[:, 0:1])
        act(dtmp[:, 192:256], dtmp[:, 64:128], AF.Identity, ["p2"], ["s2"], accum_out=s12[:, 1:2])
        act(e12, s12, AF.Exp, ["s1", "s2"], ["e12"])
        ve("dve", "tensor_tensor", ["e12"], ["lam0"], lam, e12[:, 0:1], e12[:, 1:2], ALU.subtract)
        ve("dve", "tensor_scalar", ["lam0"], ["neglam"], neglam, lam, -1.0, -0.2, ALU.mult, ALU.add)
        ve("dve", "tensor_scalar", ["gs"], ["gsub"], gsub, gsub, 0.8, 0.0, ALU.mult, ALU.add)
        act(tFall[:, 768:776], decb, AF.Exp, ["decb"], ["rt0"], scale=-1.0)
        ve("dve", "tensor_scalar", ["rt0"], ["rt1"], tFall[:, 776:784], tFall[:, 768:776], 1.0, 0.0, ALU.add, ALU.add)
        act(tFall[:, 784:792], tFall[:, 776:784], AF.Ln, ["rt1"], ["rt2"])
        ve("dve", "tensor_scalar", ["rt2"], ["lg"], lg, tFall[:, 784:792], -1.0, 0.0, ALU.mult, ALU.add)
        act(zf, lg[:, 0:4], AF.Exp, ["lg", "cs", "c1"], ["zf"], scale=jvec[:, 1:2], bias=c_ln8)
        act(zb, lg[:, 4:8], AF.Exp, ["lg", "cs", "c1"], ["zb"], scale=jvec[:, 0:1], bias=c_ln8)
        act(xif, lg[:, 0:4], AF.Exp, ["lg", "cs"], ["xif"], scale=jvec[:, 2:3])
        act(xib, lg[:, 4:8], AF.Exp, ["lg", "cs"], ["xib"], scale=jvec[:, 3:4])
        for pair in range(2):
            for hh in range(2):
                h = 2 * pair + hh
                ve("dve", "tensor_copy", ["lg"], [("lgs", pair, hh)], lgsf[hh * 64:(hh + 1) * 64, pair:pair + 1],
                   lg[hh * 64:(hh + 1) * 64, h:h + 1])
                ve("dve", "tensor_copy", ["lg"], [("lgsb", pair, hh)], lgsb[hh * 64:(hh + 1) * 64, pair:pair + 1],
                   lg[hh * 64:(hh + 1) * 64, 4 + h:5 + h])
        lgsk = [("lgs", p_, h_) for p_ in range(2) for h_ in range(2)] + [("lgsb", p_, h_) for p_ in range(2) for h_ in range(2)]
        act(gfc, lgsf, AF.Exp, lgsk, ["gfc"], scale=128.0)
        act(gbc, lgsb, AF.Exp, lgsk, ["gbc"], scale=128.0)
        for pair in range(2):
            act(gfpow[:, pair, :], idxA, AF.Exp, lgsk + ["cs"], [("gfpow", pair)], scale=lgsf[:, pair:pair + 1])
            act(gbpow[:, pair, :], idxB, AF.Exp, lgsk + ["cs"], [("gbpow", pair)], scale=lgsb[:, pair:pair + 1])
        for hh in range(2):
            for pair in range(2):
                h = 2 * pair + hh
                ve("dve", "tensor_scalar", ["lg", "cs"], ["dtA"], dtmp[:, 256:384], posm, lg[:, h:h + 1], 0.0, ALU.mult, ALU.add)
                ve("dve", "scalar_tensor_tensor", ["lg", "cs", "dtA"], ["dtB"], dtmp[:, 384:512], negm,
                   lg[:, 4 + h:5 + h], dtmp[:, 256:384], ALU.mult, ALU.add)
                act(DT[:, hh, pair * 128:(pair + 1) * 128], dtmp[:, 384:512], AF.Exp, ["dtB", "c1"], [("DT", h)], bias=c_ln8)
        ve("dve", "memset", [], ["Sf"], stS[:, :], 0.0)
        if stop_after:
            for nm_, t_ in [("RhT", RhT), ("RQT", RQT), ("RKT", RKT), ("RV1", RV1), ("RrnT", RrnT)]:
                ve("dve", "memset", [], [("dbgz", nm_)], t_[:, :], 0.0)
        kb.barrier()

        def phases():
            if stop_after == "setup":
                return
            dma(tabv, tabd[2048:4096, :].rearrange("(t p) c -> p t c", p=128), [], ["tab"])
            for t in range(16):
                s = t % 2
                dma(xt[s][:, :], xa[2048 + t * 128:2048 + (t + 1) * 128, :], [], [("xt", s)])
                norm_tile(xt[s][:, :], [("xt", s)], s, hT[:, :, t * 128:(t + 1) * 128], ("hT", t), 7)

            if stop_after == "normA":
                return

            def k_pass(slot, kname, tok_base, dstT):
                wv = wslot(slot)
                wk = wkeys(("wb", slot), 8, 512, True)
                for t in range(16):
                    b = t % 2
                    proj_tok(b, 512, t, wv, wk)
                    ob = tB[t % 3]
                    rope(b, 512, t, ob[:, 0:512], ("tB", t % 3), t % 2)
                    pb = psbf(4 + b)
                    for k in range(4):
                        tr(pb[:, k * 128:(k + 1) * 128], ob[:, k * 128:(k + 1) * 128], [("tB", t % 3)], [("ps", 4 + b)])
                    tok0 = tok_base + t * 128
                    ve("dve", "tensor_copy", [("ps", 4 + b)], [(kname, tok0)], dstT[:, :, tok0:tok0 + 128],
                       pb[:, 0:512].rearrange("p (h t) -> p h t", h=4))

            def v_pass(slot, tile_base):
                wv = wslot(slot)
                wk = wkeys(("wb", slot), 8, 512, True)
                for t in range(16):
                    b = 2 + t % 2
                    proj_tok(b, 512, t, wv, wk)
                    act(V1[:, tile_base + t, :, 0:128], ps[:, b, :].rearrange("p (h e) -> p h e", h=4), AF.Identity,
                        [("ps", b)], [("v1", tile_base + t)])

            load_w(wslot(0), ("wb", 0), w_in[:, 512:1024], 8, 512, gmix)
            load_w(wslot(1), ("wb", 1), w_in[:, 1024:1536], 8, 512, gmix)
            ve("dve", "memset", [], ["v1ones"], V1[:, :, :, 128:129], 1.0)
            if stop_after == "lw":
                return
            k_pass(0, "kt", 2048, KT)
            if stop_after == "kp":
                return
            v_pass(1, 16)
            if stop_after == "kvA":
                return
            load_w(wslot(0), ("wb", 0), w_in[:, 1792:2304], 8, 512, gmix)
            load_w(wslot(1), ("wb", 1), w_in[:, 2304:2560], 8, 256, gmix)
            wk0 = wkeys(("wb", 0), 8, 512, True)
            wk1 = wkeys(("wb", 1), 8, 256, True)

            def ret_kv(ro, rokey, rv_ap, rvkeys, ubank):
                kf = tB[2]
                ve("dve", "tensor_tensor", [rokey, "consts"], [("rkz", 0)], kf[:, 0:256].rearrange("p (h d) -> p h d", h=4),
                   ro.rearrange("p (h d) -> p h d", h=4), zf.unsqueeze(2).broadcast_to([128, 4, 64]), ALU.mult)
                ve("dve", "tensor_tensor", [rokey, "consts"], [("rkz", 1)], kf[:, 256:512].rearrange("p (h d) -> p h d", h=4),
                   ro.rearrange("p (h d) -> p h d", h=4), zb.unsqueeze(2).broadcast_to([128, 4, 64]), ALU.mult)
                for d in range(2):
                    for h in range(4):
                        pair, hh = h // 2, h % 2
                        mm(ps[hh * 64:(hh + 1) * 64, ubank, d * 256 + pair * 128:d * 256 + (pair + 1) * 128],
                           kf[:, d * 256 + h * 64:d * 256 + (h + 1) * 64], rv_ap[:, h * 128:(h + 1) * 128],
                           True, True, [("rkz", d)] + list(rvkeys), [("ps", ubank)])

            for t in range(16):
                ba, bb, bu = 0 + (t % 2), 2 + (t % 2), 6
                proj_tok(ba, 512, t, wslot(0), wk0)
                proj_tok(bb, 256, t, wslot(1), wk1)
                ro = tB[t % 2]
                rope(ba, 256, t, ro[:, 0:256], ("tB", t % 2), t % 2)
                rvo = xn[t % 2]
                act(rvo[:, 0:256], ps[:, ba, 256:512], AF.Identity, [("ps", ba)], [("xn", t % 2, 0)])
                act(rvo[:, 256:512], ps[:, bb, 0:256], AF.Identity, [("ps", bb)], [("xn", t % 2, 1)])
                ret_kv(ro[:, 0:256], ("tB", t % 2), rvo, [("xn", t % 2, 0), ("xn", t % 2, 1)], bu)
                for pair in range(2):
                    ve("dve", "scalar_tensor_tensor", [("ps", bu), "consts", "Sf"], ["Sf"],
                       Sf[:, pair * 128:(pair + 1) * 128], ps[:, bu, pair * 128:(pair + 1) * 128], gfpow[:, pair, t:t + 1],
                       Sf[:, pair * 128:(pair + 1) * 128], ALU.mult, ALU.add)
                    ve("dve", "scalar_tensor_tensor", [("ps", bu), "consts", ("Sb", 0)], [("Sb", 0)],
                       Sbp[0][:, pair * 128:(pair + 1) * 128], ps[:, bu, 256 + pair * 128:256 + (pair + 1) * 128],
                       gbpow[:, pair, t:t + 1], Sbp[0][:, pair * 128:(pair + 1) * 128], ALU.mult, ALU.add)
            ve("dve", "tensor_scalar", ["Sf", "cs"], ["Sf"], Sf, Sf, flags[:, 0:1], 0.0, ALU.mult, ALU.add)
            ve("dve", "tensor_scalar", [("Sb", 0), "cs"], [("Sb", 0)], Sbp[0], Sbp[0], flags[:, 1:2], 0.0, ALU.mult, ALU.add)
            kb.barrier()
            if stop_after == "other":
                return

            dma(tabv, tabd[0:2048, :].rearrange("(t p) c -> p t c", p=128), [], ["tab"])
            for t in range(16):
                s = t % 2
                dma(xt[s][:, :], xa[t * 128:(t + 1) * 128, :], [], [("xt", s)])
                norm_tile(xt[s][:, :], [("xt", s)], s, hT[:, :, t * 128:(t + 1) * 128], ("hT", t), 7)

            load_w(wslot(0), ("wb", 0), w_in[:, 1536:2048], 8, 512, gmix)
            load_w(wslot(1), ("wb", 1), w_in[:, 2048:2560], 8, 512, gmix)
            wk0 = wkeys(("wb", 0), 8, 512, True)
            wk1 = wkeys(("wb", 1), 8, 512, True)
            RQK = RQT[:, :].rearrange("p (k t) -> p k t", k=4)

            def rv_tile(i):
                return rvs[:, i // 4, (i % 4) * 512:(i % 4 + 1) * 512]

            for i in range(16):
                ba, bb, bu = 0 + (i % 2), 2 + (i % 2), 6
                proj_tok(ba, 512, i, wslot(0), wk0)
                proj_tok(bb, 512, i, wslot(1), wk1)
                ro = tB[i % 2]
                rope(ba, 512, i, ro[:, 0:512], ("tB", i % 2), i % 2)
                act(rv_tile(i), ps[:, bb, :], AF.Identity, [("ps", bb)], [("rv", i)])
                pb = psbf(4 + i % 2)
                for k in range(4):
                    tr(pb[:, k * 128:(k + 1) * 128], ro[:, k * 128:(k + 1) * 128], [("tB", i % 2)], [("ps", 4 + i % 2)])
                ve("dve", "tensor_copy", [("ps", 4 + i % 2)], [("rqk", i)], RQK[:, :, i * 128:(i + 1) * 128],
                   pb[:, 0:512].rearrange("p (k t) -> p k t", k=4))
                ret_kv(ro[:, 256:512], ("tB", i % 2), rv_tile(i), [("rv", i)], bu)
                act(Sfb[:, i, :], Sf, AF.Identity, ["Sf"], [("Sfb", i)])
                for pair in range(2):
                    ve("dve", "scalar_tensor_tensor", [("ps", bu), "Sf", "consts"], ["Sf"],
                       Sf[:, pair * 128:(pair + 1) * 128], Sf[:, pair * 128:(pair + 1) * 128], gfc[:, pair:pair + 1],
                       ps[:, bu, pair * 128:(pair + 1) * 128], ALU.mult, ALU.add)
                act(Ubs[:, i, :], ps[:, bu, 256:512], AF.Identity, [("ps", bu)], [("Ubs", i)])
            cur = 0
            for i in range(15, -1, -1):
                nxt = 1 - cur
                for pair in range(2):
                    ve("dve", "scalar_tensor_tensor", [("Ubs", i), ("Sb", cur), ("Sb", nxt), "consts"], [("Sb", nxt)],
                       Sbp[nxt][:, pair * 128:(pair + 1) * 128], Sbp[cur][:, pair * 128:(pair + 1) * 128], gbc[:, pair:pair + 1],
                       Ubs[:, i, pair * 128:(pair + 1) * 128], ALU.mult, ALU.add)
                act(Ubs[:, i, :], Sbp[cur], AF.Identity, [("Sb", cur), ("Sb", nxt)], [("Ubs", i)])
                cur = nxt
            for i in range(16):
                bS = [0, 1]
                bR = 2
                bC = [3, 4]
                tsl = slice(i * 128, (i + 1) * 128)
                for pair in range(2):
                    for hh in range(2):
                        rows = slice(hh * 64, (hh + 1) * 64)
                        mm(ps[:, bS[hh], pair * 128:(pair + 1) * 128], RQK[rows, 2 + pair, tsl], RQK[rows, pair, tsl],
                           True, True, [("rqk", i)], [("ps", bS[hh])])
                PT = tB[i % 2][:, 0:512].rearrange("p (a b) -> p a b", a=2)
                for hh in range(2):
                    ve("dve", "tensor_tensor", [("ps", bS[hh]), "consts"], [("PT", i % 2, hh)], PT[:, hh, :], ps[:, bS[hh], 0:256],
                       DT[:, hh, :], ALU.mult)
                for h in range(4):
                    pair, hh = h // 2, h % 2
                    mm(ps[:, bR, h * 128:(h + 1) * 128], PT[:, hh, pair * 128:(pair + 1) * 128], rv_tile(i)[:, h * 128:(h + 1) * 128],
                       True, True, [("PT", i % 2, hh), ("rv", i)], [("ps", bR)])
                for d in range(2):
                    St = Sfb if d == 0 else Ubs
                    for h in range(4):
                        pair, hh = h // 2, h % 2
                        rows = slice(hh * 64, (hh + 1) * 64)
                        mm(ps[:, bC[hh], d * 256 + pair * 128:d * 256 + (pair + 1) * 128], RQK[rows, pair, tsl],
                           St[rows, i, pair * 128:(pair + 1) * 128], True, True,
                           [("rqk", i), ("Sfb", i), ("Ubs", i)], [("ps", bC[hh])])
                rt_ = tF[i % 2]
                act(rt_[:, :], ps[:, bR, :], AF.Identity, [("ps", bR)], [("rt", i % 2)])
                k = nslot()
                for h in range(4):
                    pair, hh = h // 2, h % 2
                    ve("dve", "scalar_tensor_tensor", [("ps", bC[hh]), ("rt", i % 2), "consts"], [("rt", i % 2, h)],
                       rt_[:, h * 128:(h + 1) * 128], ps[:, bC[hh], pair * 128:(pair + 1) * 128], xif[:, h:h + 1],
                       rt_[:, h * 128:(h + 1) * 128], ALU.mult, ALU.add)
                    ve("dve", "scalar_tensor_tensor", [("ps", bC[hh]), ("rt", i % 2, h), "consts"], [("rt", i % 2, h)],
                       rt_[:, h * 128:(h + 1) * 128], ps[:, bC[hh], 256 + pair * 128:256 + (pair + 1) * 128], xib[:, h:h + 1],
                       rt_[:, h * 128:(h + 1) * 128], ALU.mult, ALU.add)
                    act(junk[:, 0:128], rt_[:, h * 128:(h + 1) * 128], AF.Square, [("rt", i % 2, h)], ["junk", ("nrm", k, "s", h)],
                        accum_out=nrm[:, k, h:h + 1])
                act(nrm[:, k, 4:8], nrm[:, k, 0:4], AF.Sqrt, [("nrm", k, "s", h_) for h_ in range(4)] + ["consts"], [("nrm", k, 1)],
                    scale=1.0 / 128, bias=c_eps)
                ve("dve", "reciprocal", [("nrm", k, 1)], [("nrm", k, 2)], nrm[:, k, 8:12], nrm[:, k, 4:8])
                rnb = xn[i % 2]
                for h in range(4):
                    ve("dve", "tensor_scalar", [("rt", i % 2, h), ("nrm", k, 2)], [("rnb", i % 2, h)], rnb[:, h * 128:(h + 1) * 128],
                       rt_[:, h * 128:(h + 1) * 128], nrm[:, k, 8 + h:9 + h], 0.0, ALU.mult, ALU.add)
                pb = psbf(5)
                for h in range(4):
                    tr(pb[:, h * 128:(h + 1) * 128], rnb[:, h * 128:(h + 1) * 128], [("rnb", i % 2, h)], [("ps", 5)])
                ve("dve", "tensor_copy", [("ps", 5)], [("rnT", i)], rnT[:, :, tsl], pb[:, 0:512].rearrange("p (h t) -> p h t", h=4))
            kb.barrier()
            if stop_after == "ret":
                return

            load_w(wslot(0), ("wb", 0), w_in[:, 0:512], 8, 512, gmix)
            load_w(wslot(1), ("wb", 1), w_in[:, 512:1024], 8, 512, gmix)
            ve("dve", "memset", [], ["v1ones"], V1[:, 0:16, :, 128:129], 1.0)
            k_pass(0, "qt", 0, QT)
            k_pass(1, "kt", 0, KT)
            load_w(wslot(0), ("wb", 0), w_in[:, 1024:1536], 8, 512, gmix)
            v_pass(0, 0)
            kb.barrier()
            if stop_after == "own1":
                return

            def acc_ap(slot):
                b, c0 = slot // 3, (slot % 3) * 129
                return ps[:, b, c0:c0 + 129], b

            for h in range(4):
                for qb in range(4):
                    for b in range(3):
                        mm(ps[:, b, :], zerob[:, 0:128], zerob[:, :], True, True, ["zerob"], [("ps", b)])
                    qs = slice(qb * 512, (qb + 1) * 512)
                    for kt in range(32):
                        par = kt % 2
                        ks = slice(kt * 128, (kt + 1) * 128)
                        kkey = ("kt", kt * 128)
                        qkeys = [("qt", qb * 512 + j * 128) for j in range(4)]
                        mm(ps[:, 3 + 2 * par, :], KT[0:64, h, ks], QT[0:64, h, qs], True, True, [kkey] + qkeys, [("ps", 3 + 2 * par)])
                        mm(ps[:, 4 + 2 * par, :], KT[64:128, h, ks], QT[64:128, h, qs], True, True, [kkey] + qkeys, [("ps", 4 + 2 * par)])
                        E = tB[kt % 3]
                        act(E[:, :].rearrange("p (c q) -> p c q", c=2), ps[:, 3 + 2 * par:5 + 2 * par, :], AF.Exp,
                            [("ps", 3 + 2 * par), ("ps", 4 + 2 * par)], [("tB", kt % 3)], scale=0.125)
                        for c in range(2):
                            for j in range(4):
                                a, b = acc_ap(c * 4 + j)
                                mm(a, E[:, c * 512 + j * 128:c * 512 + (j + 1) * 128], V1[:, kt, h, :], False, kt == 31,
                                   [("tB", kt % 3), ("v1", kt), "v1ones"], [("ps", b)], skip_group_check=True)
                    k = nslot()
                    rec = nrm[:, k, 0:8]
                    for b in range(3):
                        n = 3 if b < 2 else 2
                        ve("dve", "reciprocal", [("ps", b)], [("nrm", k, "r", b)], rec[:, 3 * b:3 * b + n],
                           ps[:, b, 0:129 * n].rearrange("p (s c) -> p s c", c=129)[:, :, 128])
                    rk_ = [("nrm", k, "r", b) for b in range(3)]
                    ve("dve", "tensor_scalar", rk_ + ["consts"], [("nrm", k, "nl")], nrm[:, k, 8:12], rec[:, 4:8], neglam, 0.0, ALU.mult, ALU.add)
                    af = [tF[0], tF[1]]
                    for j in range(4):
                        a0, b0 = acc_ap(j)
                        a1, b1 = acc_ap(4 + j)
                        dst = af[j // 2][:, (j % 2) * 128:(j % 2 + 1) * 128]
                        ve("dve", "tensor_scalar", [("ps", b0)] + rk_, [("af", j)], dst, a0[:, 0:128], rec[:, j:j + 1], 0.0, ALU.mult, ALU.add)
                        ve("dve", "scalar_tensor_tensor", [("ps", b1), ("af", j), ("nrm", k, "nl")], [("af", j)], dst, a1[:, 0:128],
                           nrm[:, k, 8 + j:9 + j], dst, ALU.mult, ALU.add)
                    for jj in range(2):
                        ve("dve", "tensor_tensor", [("af", 2 * jj), ("af", 2 * jj + 1)], [("afs", jj)], af[jj][:, 256:512], af[jj][:, 0:256],
                           af[jj][:, 0:256], ALU.mult)
                        ve("dve", "tensor_reduce", [("afs", jj)], [("nrm", k, "ss", jj)], nrm[:, k, 12 + 2 * jj:14 + 2 * jj],
                           af[jj][:, 256:512].rearrange("p (j e) -> p j e", j=2), mybir.AxisListType.X, ALU.add)
                    k2 = nslot()
                    act(nrm[:, k2, 0:4], nrm[:, k, 12:16], AF.Sqrt, [("nrm", k, "ss", j) for j in range(2)] + ["consts"], [("nrm", k2, 1)],
                        scale=1.0 / 128, bias=c_eps)
                    ve("dve", "reciprocal", [("nrm", k2, 1)], [("nrm", k2, 2)], nrm[:, k2, 4:8], nrm[:, k2, 0:4])
                    anb = xn[(h * 4 + qb) % 2]
                    for j in range(4):
                        dst = af[j // 2][:, (j % 2) * 128:(j % 2 + 1) * 128]
                        ve("dve", "tensor_scalar", [("af", j), ("nrm", k2, 2)], [("anb", j)], anb[:, j * 128:(j + 1) * 128], dst,
                           nrm[:, k2, 4 + j:5 + j], 0.0, ALU.mult, ALU.add)
                    pb = psbf(7)
                    for j in range(4):
                        tr(pb[:, j * 128:(j + 1) * 128], anb[:, j * 128:(j + 1) * 128], [("anb", j)], [("ps", 7)])
                    ve("dve", "tensor_copy", [("ps", 7)], [("anT", h, qb)],
                       QT[:, h, qs], pb[:, 0:512])
            kb.barrier()
            if stop_after == "att":
                return

            load_w(wslot(0), ("wb", 0), w_in[:, 2560:3072], 8, 512, gmix)
            wk0 = wkeys(("wb", 0), 8, 512, True)
            n = 0
            for h in range(4):
                for tb in range(4):
                    b = n % 4
                    for c in range(8):
                        mm(ps[:, b, :], wslot(0)[:, c, h * 128:(h + 1) * 128], hT[:, c, tb * 512:(tb + 1) * 512], c == 0, c == 7,
                           wk0 + [("hT", tb * 4 + j) for j in range(4)], [("ps", b)])
                    sg = tB[n % 3]
                    act(sg[:, 0:512], ps[:, b, :], AF.Silu, [("ps", b)], [("tB", n % 3)])
                    ve("dve", "tensor_tensor", [("tB", n % 3)], [("rnTg", h, tb)], rnT[:, h, tb * 512:(tb + 1) * 512],
                       rnT[:, h, tb * 512:(tb + 1) * 512], sg[:, 0:512], ALU.mult)
                    n += 1
            for sub in range(2):
                for g2 in range(2):
                    gcol = (3072 if sub == 0 else 4096) + g2 * 512
                    load_w(wslot(0), ("wb", 0), w_in[:, gcol:gcol + 512], 8, 512, gmix)
                    wup = w_upa if sub == 0 else w_upb
                    load_w(wslot(1, 4)[:, :, 0:512], ("wb", 1), wup[:, g2 * 512:(g2 + 1) * 512], 4, 512, gsub if sub == 0 else None)
                    wk0 = wkeys(("wb", 0), 8, 512, True)
                    wk1 = wkeys(("wb", 1), 4, 512, sub == 0)
                    srcT = QT if sub == 0 else rnT
                    for ti in range(16):
                        b1, b2 = (ti % 2) * 2, (ti % 2) * 2 + 1
                        tsl = slice(ti * 128, (ti + 1) * 128)
                        proj_tok(b1, 512, ti, wslot(0), wk0)
                        for h in range(4):
                            mm(ps[:, b2, :], srcT[:, h, tsl], wslot(1, 4)[:, h, 0:512], h == 0, h == 3,
                               wk1 + ([("rnTg", h, ti // 4)] if sub == 1 else []), [("ps", b2)])
                        sgm = tF[ti % 2]
                        act(sgm[:, :], ps[:, b1, :], AF.Sigmoid, [("ps", b1)], [("sgm", ti % 2)])
                        if sub == 0:
                            ve("dve", "tensor_tensor", [("sgm", ti % 2), ("ps", b2)], [("mtok", ti, g2)],
                               mtok[:, ti, g2 * 512:(g2 + 1) * 512], sgm[:, :], ps[:, b2, :], ALU.mult)
                        else:
                            ve("dve", "tensor_tensor", [("sgm", ti % 2), ("ps", b2)], [("sgm", ti % 2)], sgm[:, :], sgm[:, :], ps[:, b2, :], ALU.mult)
                            mf = tB[ti % 2]
                            ve("dve", "tensor_tensor", [("sgm", ti % 2), ("mtok", ti, g2)], [("tB", ti % 2)], mf[:, 0:512], sgm[:, :],
                               mtok[:, ti, g2 * 512:(g2 + 1) * 512], ALU.add)
                            pb = psbf(4 + ti % 2)
                            for k in range(4):
                                tr(pb[:, k * 128:(k + 1) * 128], mf[:, k * 128:(k + 1) * 128], [("tB", ti % 2)], [("ps", 4 + ti % 2)])
                            ve("dve", "tensor_copy", [("ps", 4 + ti % 2)], [("mT", ti, g2)], mT[:, g2 * 4:(g2 + 1) * 4, tsl],
                               pb[:, 0:512].rearrange("p (c t) -> p c t", c=4))
            kb.barrier()
            if stop_after == "d1":
                return

            load_w(wslot(0), ("wb", 0), w_o[:, 0:512], 8, 512, None)
            load_w(wslot(1), ("wb", 1), w_o[:, 512:1024], 8, 512, None)
            wko = [wkeys(("wb", 0), 8, 512, False), wkeys(("wb", 1), 8, 512, False)]
            for ti in range(16):
                s = ti % 2
                tsl = slice(ti * 128, (ti + 1) * 128)
                dma(xt[s][:, :], xa[ti * 128:(ti + 1) * 128, :], [], [("xt", s)])
                for half in range(2):
                    b = (ti % 2) * 2 + half
                    for c in range(8):
                        mm(ps[:, b, :], mT[:, c, tsl], wslot(half)[:, c, :], c == 0, c == 7,
                           wko[half] + [("mT", ti, 0), ("mT", ti, 1)], [("ps", b)])
                    ve("dve", "tensor_tensor", [("ps", b), ("xt", s)], [("x1", ti, half)], x1(ti)[:, half * 512:(half + 1) * 512],
                       xt[s][:, half * 512:(half + 1) * 512], ps[:, b, :], ALU.add)
                norm_tile(x1(ti), [("x1", ti, 0), ("x1", ti, 1)], s, mT[:, :, tsl], ("mT", ti, 0), 6 + ti % 2)
                kb.last_w[("mT", ti, 1)] = kb.last_w[("mT", ti, 0)]
            kb.barrier()
            if stop_after == "d2":
                return

            dma(gfb, g_fin.partition_broadcast(128), [], ["gfb"])
            slotsA = [Rw[0][:, :], Rw[1][:, :], RQT[:, 0:4096]]
            slotsB = [RQT[:, 4096:8192], RrnT[:, 0:4096], RrnT[:, 4096:8192]]
            nd = 0
            for fg in range(6):
                ncol = 512 if fg < 5 else 256
                nch = ncol // 128
                sl = slotsA if fg % 2 == 0 else slotsB
                sid = fg % 2
                Wg = sl[0][:, 0:4096].rearrange("p (c n) -> p c n", c=8)
                Wu = sl[1][:, 0:4096].rearrange("p (c n) -> p c n", c=8)
                Wd = sl[2][:, 0:4096].rearrange("p (c n) -> p c n", c=4)
                load_w(Wg[:, :, 0:ncol], ("fw", sid, 0), w_fg[:, fg * 512:fg * 512 + ncol], 8, ncol, gffn)
                load_w(Wu[:, :, 0:ncol], ("fw", sid, 1), w_fu[:, fg * 512:fg * 512 + ncol], 8, ncol, gffn)
                load_w(Wd[:, 0:nch, :], ("fw", sid, 2), w_fd[fg * 512:fg * 512 + ncol, :], nch, 1024, None)
                kg = wkeys(("fw", sid, 0), 8, ncol, True)
                ku = wkeys(("fw", sid, 1), 8, ncol, True)
                kd = wkeys(("fw", sid, 2), nch, 1024, False)
                for tb in range(4):
                    actT = (xn[0] if tb % 2 == 0 else xn[1])
                    actT2 = (tB[0] if tb % 2 == 0 else tB[1])
                    hk = [("mT", tb * 4 + j, 0) for j in range(4)]
                    for ch in range(nch):
                        bg, bu = (ch % 2) * 2, (ch % 2) * 2 + 1
                        for c in range(8):
                            mm(ps[:, bg, :], Wg[:, c, ch * 128:(ch + 1) * 128], mT[:, c, tb * 512:(tb + 1) * 512], c == 0, c == 7,
                               kg + hk, [("ps", bg)])
                        for c in range(8):
                            mm(ps[:, bu, :], Wu[:, c, ch * 128:(ch + 1) * 128], mT[:, c, tb * 512:(tb + 1) * 512], c == 0, c == 7,
                               ku + hk, [("ps", bu)])
                        sgf = tF[ch % 2]
                        act(sgf[:, :], ps[:, bg, :], AF.Silu, [("ps", bg)], [("sgf", ch % 2)])
                        dstA = (actT if ch < 2 else actT2)[:, (ch % 2) * 512:(ch % 2 + 1) * 512]
                        ve("dve", "tensor_tensor", [("sgf", ch % 2), ("ps", bu)], [("actT", tb % 2, ch)], dstA, sgf[:, :], ps[:, bu, :], ALU.mult)
                    for tt in range(4):
                        ti = tb * 4 + tt
                        for half in range(2):
                            bd = 4 + nd % 4
                            nd += 1
                            for ch in range(nch):
                                srcA = (actT if ch < 2 else actT2)[:, (ch % 2) * 512 + tt * 128:(ch % 2) * 512 + (tt + 1) * 128]
                                mm(ps[:, bd, :], srcA, Wd[:, ch, half * 512:(half + 1) * 512], ch == 0, ch == nch - 1,
                                   kd + [("actT", tb % 2, ch)], [("ps", bd)])
                            ve("dve", "tensor_tensor", [("ps", bd)], [("x1", ti, half)], x1(ti)[:, half * 512:(half + 1) * 512],
                               x1(ti)[:, half * 512:(half + 1) * 512], ps[:, bd, :], ALU.add)
            for ti in range(16):
                s = ti % 2
                k = nslot()
                ss, rt, rs = nrm[:, k, 0:1], nrm[:, k, 1:2], nrm[:, k, 2:3]
                act(junk[:, :], x1(ti), AF.Square, [("x1", ti, 0), ("x1", ti, 1)], ["junk", ("nrm", k)], accum_out=ss)
                act(rt, ss, AF.Sqrt, [("nrm", k), "consts"], [("nrm", k, 1)], scale=1.0 / 1024, bias=c_eps)
                ve("dve", "reciprocal", [("nrm", k, 1)], [("nrm", k, 2)], rs, rt)
                ve("dve", "scalar_tensor_tensor", [("x1", ti, 0), ("x1", ti, 1), ("nrm", k, 2), "gfb"], [("xt", s)], xt[s][:, :], x1(ti), rs,
                   gfb, ALU.mult, ALU.mult)
                dma(y[ti * 128:(ti + 1) * 128, :], xt[s][:, :], [("xt", s)], [("yout", ti % 4)])
            kb.add("sp", lambda e: e.nop(), [("yout", i) for i in range(4)], [])


        phases()
        if stop_after:
            kb.barrier()
            names = []
            lst = [("RhT", RhT[:, :]), ("RQT", RQT[:, :]), ("RKT", RKT[:, :]), ("RV1", RV1[:, :]), ("RrnT", RrnT[:, :]),
                   ("stS", stS[:, :]), ("sm", sm[:, :]), ("rett", rett[:, :])]
            if stop_after == "d2":
                lst = [("x1a", RhT[:, :].bitcast(F32)), ("x1b", RKT[:, :].bitcast(F32)), ("RV1", RV1[:, :])]
            for name, t in lst:
                d = nc.dram_tensor("dbg_" + name, list(t.shape), t.dtype, kind="ExternalOutput").ap()
                dma(d, t, [], [("dbg", name)])
                names.append(("dbg", name))
            kb.add("sp", lambda e: e.nop(), names, [])
        kb.emit(es)
    return nc


_NC_CACHE = {}


def _host_consts():
    c = np.zeros((128, 424), np.float32)
    c[:, 0:128] = np.eye(128, dtype=np.float32)
    j = np.arange(128, dtype=np.float32)
    diff = j[None, :] - j[:, None]
    c[:, 128:256] = np.maximum(diff, 0)
    c[:, 256:384] = np.maximum(-diff, 0)
    c[:, 384] = j
    c[:, 385] = 127 - j
    c[:, 386] = j + 1
    c[:, 387] = 128 - j
    jj = np.arange(16, dtype=np.float32)
    c[:, 388:404] = (128.0 * (15 - jj))[None, :]
    c[:, 404:420] = (128.0 * jj)[None, :]
    return c


def _rope_table():
    inv_freq = (10000.0 ** (-np.arange(0, 64, 2, dtype=np.float32) / np.float32(64))).astype(np.float32)
    ang = np.arange(4096, dtype=np.float32)[:, None] * inv_freq[None, :]
    cos = np.cos(ang).astype(np.float32)
    sin = np.sin(ang).astype(np.float32)
    return np.concatenate([cos, sin, -sin], axis=1).astype(np.float32)


def make_in_maps(inputs, small=False):
    f = lambda a: np.ascontiguousarray(np.asarray(a, dtype=np.float32))
    x = f(inputs["x"])
    tabfull = _rope_table()
    cbase = _host_consts()
    shared = {
        "w_in": f(inputs["w_in"][0]), "w_up_diff": f(inputs["w_up_diff"][0]), "w_up_ret": f(inputs["w_up_ret"][0]),
        "w_o": f(inputs["w_o"][0]), "w_ffn_gate": f(inputs["w_ffn_gate"][0]), "w_ffn_up": f(inputs["w_ffn_up"][0]),
        "w_ffn_down": f(inputs["w_ffn_down"][0]), "g_mix": f(np.asarray(inputs["g_mix"][0]).reshape(8, 128).T), "g_ffn": f(np.asarray(inputs["g_ffn"][0]).reshape(8, 128).T),
        "g_final": f(inputs["g_final"]), "diff_subln_g": f(np.asarray(inputs["diff_subln_g"][0]).reshape(128, 1)),
        "lqk": f(np.concatenate([inputs["diff_lq1"][0], inputs["diff_lk1"][0], inputs["diff_lq2"][0], inputs["diff_lk2"][0]], 0)),
        "dec": f(np.concatenate([inputs["ret_decay_fwd"][0], inputs["ret_decay_bwd"][0]], 0)),
    }
    if small:
        for k_ in ("w_ffn_gate", "w_ffn_up", "w_ffn_down"):
            shared[k_] = np.zeros((1, 1), np.float32)
    maps = []
    for core in range(8):
        b, half = core // 2, core % 2
        own = slice(half * 2048, (half + 1) * 2048)
        oth = slice((1 - half) * 2048, (2 - half) * 2048)
        xa = np.ascontiguousarray(np.concatenate([x[b, own], x[b, oth]], 0))
        tab = np.ascontiguousarray(np.concatenate([tabfull[own], tabfull[oth]], 0))
        c = cbase.copy()
        c[:, 420] = 1.0 if half == 1 else 0.0
        c[:, 421] = 1.0 if half == 0 else 0.0
        m = dict(shared)
        m.update({"xa": xa, "tab": tab, "cst": c})
        maps.append(m)
    return maps


def kernel(**inputs):
    if "nc" not in _NC_CACHE:
        _NC_CACHE["nc"] = build()
    nc = _NC_CACHE["nc"]
    maps = make_in_maps(inputs)
    res = run_bass_kernel_spmd(nc, maps, core_ids=list(range(8)))
    out = np.zeros((4, 4096, 1024), np.float32)
    for core in range(8):
        b, half = core // 2, core % 2
        out[b, half * 2048:(half + 1) * 2048] = np.asarray(res.results[core]["y"], dtype=np.float32)
    return out
```

```python
import math
from contextlib import ExitStack

import numpy as np
import concourse.bass as bass
import concourse.mybir as mybir
from concourse.bass_utils import run_bass_kernel_spmd

F32 = mybir.dt.float32
BF16 = mybir.dt.bfloat16
AF = mybir.ActivationFunctionType
ALU = mybir.AluOpType

P = 128
NOWN = 16
EPS = 1e-5
LN8 = math.log(0.125)
SAME_ENGINE_SYNC = True
SERIALIZE = True
SER_ENGS = ("pe", "act", "dve", "pool")
import os
ROPE_ADD_ENG = os.environ.get("ROPE_ADD_ENG", "dve")
EARLY = ("setup", "normA", "lw", "kp", "kvA", "other", "ret", "own1", "att")


class _Op:
    __slots__ = ("eng", "fn", "dma", "deps", "waits", "inc", "val", "sem", "idx", "wkeys", "phase")


class KB:
    COMPUTE = ("pe", "act", "dve", "pool")

    def __init__(self, nc):
        self.nc = nc
        self.ops = []
        self.last_w = {}
        self.readers = {}
        self.phase = 0
        self._last_barrier = 0

    def add(self, eng, fn, reads=(), writes=(), dma=False, extra_deps=()):
        op = _Op()
        op.eng, op.fn, op.dma = eng, fn, dma
        op.idx = len(self.ops)
        op.phase = self.phase
        deps = set(extra_deps)
        if SERIALIZE and eng in SER_ENGS:
            prev = getattr(self, "_last_ser", None)
            if prev is not None:
                deps.add(prev)
            self._last_ser = op.idx
        for k in reads:
            w = self.last_w.get(k)
            if w is not None:
                deps.add(w)
        for k in writes:
            w = self.last_w.get(k)
            if w is not None:
                deps.add(w)
            deps.update(self.readers.get(k, ()))
        for k in reads:
            self.readers.setdefault(k, []).append(op.idx)
        for k in writes:
            self.last_w[k] = op.idx
            self.readers[k] = []
        deps.discard(op.idx)
        op.deps = deps
        op.waits = []
        op.wkeys = []
        op.inc = False
        op.val = 0
        op.sem = writes[0] if dma else None
        self.ops.append(op)
        return op.idx

    def barrier(self):
        last = {}
        for op in self.ops[self._last_barrier:]:
            last[op.eng] = op.idx
        deps = list(last.values())
        deps += [op.idx for op in self.ops[self._last_barrier:] if op.dma]
        for e in ("pe", "act", "dve", "pool", "sp"):
            self.add(e, lambda en: en.nop(), extra_deps=deps)
        self._last_barrier = len(self.ops)
        self.phase += 1

    def _eff_deps(self, op):
        ops = self.ops
        best = {}
        out = []
        for d in op.deps:
            dop = ops[d]
            if dop.dma:
                out.append(dop)
                continue
            if dop.eng == "sp":
                continue
            if dop.phase != op.phase:
                continue
            if dop.eng == op.eng and (dop.eng == "pe" or not SAME_ENGINE_SYNC) and not op.dma:
                continue
            if dop.eng not in best or best[dop.eng].idx < dop.idx:
                best[dop.eng] = dop
        return out + list(best.values())

    def emit(self, es):
        nc = self.nc
        ops = self.ops
        eff = [self._eff_deps(op) for op in ops]
        for lst in eff:
            for dop in lst:
                if not dop.dma:
                    dop.inc = True
        esem = {}
        dsem = {}
        cnt = {}
        dcnt = {}
        for op in ops:
            if op.dma:
                k = op.sem
                if k not in dsem:
                    dsem[k] = es.enter_context(nc.semaphore("d%d" % len(dsem)))
                    dcnt[k] = 0
                dcnt[k] += 16
                op.val = dcnt[k]
            elif op.inc:
                k = (op.eng, op.phase)
                if k not in esem:
                    esem[k] = es.enter_context(nc.semaphore("s_%s%d" % k))
                    cnt[k] = 0
                cnt[k] += 1
                op.val = cnt[k]
        seen = {}
        for op, lst in zip(ops, eff):
            need = {}
            for dop in lst:
                if dop.dma:
                    s = dsem[dop.sem]
                    key = ("d", dop.sem)
                else:
                    s = esem[(dop.eng, dop.phase)]
                    key = ("e", dop.eng, dop.phase)
                if dop.val > need.get(key, (None, 0))[1]:
                    need[key] = (s, dop.val)
            sm = seen.setdefault(op.eng, {})
            for key, (s, v) in need.items():
                if sm.get(key, 0) >= v:
                    continue
                sm[key] = v
                op.waits.append((s, v))
                op.wkeys.append((key, v))
        by = {}
        for op in ops:
            by.setdefault(op.eng, []).append(op)
        semv = {}
        ptr = {e: 0 for e in by}
        progress = True
        while progress:
            progress = False
            for e, lst in by.items():
                while ptr[e] < len(lst):
                    op = lst[ptr[e]]
                    if all(semv.get(k, 0) >= v for (k, v) in op.wkeys):
                        if op.dma:
                            semv[("d", op.sem)] = semv.get(("d", op.sem), 0) + 16
                        elif op.inc:
                            k = ("e", op.eng, op.phase)
                            semv[k] = semv.get(k, 0) + 1
                        ptr[e] += 1
                        progress = True
                    else:
                        break
        stuck = {e: (ptr[e], len(lst)) for e, lst in by.items() if ptr[e] < len(lst)}
        if stuck:
            msg = []
            for e in stuck:
                op = by[e][ptr[e]]
                msg.append((e, op.idx, op.wkeys, {k: semv.get(k, 0) for k, _ in op.wkeys}))
            raise RuntimeError("DEADLOCK in wait graph: %r" % (msg,))
        print("kb: ops", {e: len(l) for e, l in by.items()}, "sems", len(esem) + len(dsem),
              "incs", {e: sum(1 for o in l if o.inc) for e, l in by.items()}, "maxval", max(list(cnt.values()) + [0]))
        block = es.enter_context(nc.Block())

        def run(engine, lst):
            for op in lst:
                for (s, v) in op.waits:
                    engine.wait_ge(s, v)
                ins = op.fn(engine)
                if op.dma:
                    ins.then_inc(dsem[op.sem], 16)
                elif op.inc:
                    ins.then_inc(esem[(op.eng, op.phase)], 1)

        @block.tensor
        def _(e):
            run(e, by.get("pe", []))

        @block.scalar
        def _(e):
            run(e, by.get("act", []))

        @block.vector
        def _(e):
            run(e, by.get("dve", []))

        @block.gpsimd
        def _(e):
            run(e, by.get("pool", []))

        @block.sync
        def _(e):
            run(e, by.get("sp", []))


def build(stop_after=None):
    nc = bass.Bass("TRN2", target_bir_lowering=False)

    def din(name, shape):
        return nc.dram_tensor(name, list(shape), F32, kind="ExternalInput").ap()

    xa = din("xa", [4096, 1024])
    tabd = din("tab", [4096, 96])
    cst = din("cst", [128, 424])
    w_in = din("w_in", [1024, 5120])
    w_upa = din("w_up_diff", [512, 1024])
    w_upb = din("w_up_ret", [512, 1024])
    w_o = din("w_o", [1024, 1024])
    small = stop_after in EARLY
    w_fg = din("w_ffn_gate", [1024, 2816] if not small else [1, 1])
    w_fu = din("w_ffn_up", [1024, 2816] if not small else [1, 1])
    w_fd = din("w_ffn_down", [2816, 1024] if not small else [1, 1])
    g_mix = din("g_mix", [128, 8])
    g_ffn = din("g_ffn", [128, 8])
    g_fin = din("g_final", [1024])
    g_sub = din("diff_subln_g", [128, 1])
    lqk = din("lqk", [256])
    dec = din("dec", [8])
    y = nc.dram_tensor("y", [2048, 1024], F32, kind="ExternalOutput").ap()
    dbg_out = {}

    kb = KB(nc)
    with ExitStack() as es:
        def sb(name, shape, dt):
            return es.enter_context(nc.sbuf_tensor(name, list(shape), dt))

        ps = es.enter_context(nc.psum_tensor("ps", [128, 8, 512], F32))

        cs = sb("csb", [128, 424], F32)
        identb = sb("identb", [128, 128], BF16)
        zerob = sb("zerob", [128, 512], BF16)
        tab = sb("tabs", [128, 16 * 96], F32)
        tabv = tab[:, :].rearrange("p (t c) -> p t c", c=96)
        sm = sb("sm", [128, 256], F32)
        rett = sb("rett", [128, 512], F32)
        stS = sb("stS", [128, 4 * 256], F32)
        xt = [sb("xt%d" % i, [128, 1024], F32) for i in range(2)]
        xn = [sb("xn%d" % i, [128, 1024], BF16) for i in range(2)]
        junk = sb("junk", [128, 1024], BF16)
        tFall = sb("tFall", [128, 2048], F32)
        tF = [tFall[:, i * 512:(i + 1) * 512] for i in range(4)]
        gfb = tFall[:, 1024:2048]
        tB = [sb("tB%d" % i, [128, 1024], BF16) for i in range(3)]
        stg = [sb("stg%d" % i, [128, 2048], F32) for i in range(2)]
        Rw = [sb("Rw%d" % i, [128, 4096], BF16) for i in range(2)]
        RhT = sb("RhT", [128, 16384], BF16)
        RQT = sb("RQT", [128, 8192], BF16)
        RKT = sb("RKT", [128, 16384], BF16)
        RV1 = sb("RV1", [128, 32 * 516], BF16)
        RrnT = sb("RrnT", [128, 8192], BF16)

        hT = RhT[:, :].rearrange("p (c t) -> p c t", c=8)
        QT = RQT[:, :].rearrange("p (h t) -> p h t", h=4)
        KT = RKT[:, :].rearrange("p (h t) -> p h t", h=4)
        V1 = RV1[:, :].rearrange("p (t h e) -> p t h e", t=32, h=4)
        rnT = RrnT[:, :].rearrange("p (h t) -> p h t", h=4)
        rvs = RKT[:, :].rearrange("p (h t) -> p h t", h=4)
        Sfb = RV1[:, 0:4096].rearrange("p (t c) -> p t c", c=256)
        Ubs = RV1[:, 4096:8192].rearrange("p (t c) -> p t c", c=256)
        mtok = RKT[:, :].rearrange("p (t c) -> p t c", c=1024)
        mT = RV1[:, 0:16384].rearrange("p (c t) -> p c t", c=8)
        x1a = RhT[:, :].bitcast(F32).rearrange("p (t c) -> p t c", c=1024)
        x1b = RKT[:, :].bitcast(F32).rearrange("p (t c) -> p t c", c=1024)

        def x1(ti):
            return x1a[:, ti, :] if ti < 8 else x1b[:, ti - 8, :]

        def x1key(ti):
            return ("x1", ti)

        ident_f = cs[:, 0:128]
        posm = cs[:, 128:256]
        negm = cs[:, 256:384]
        jvec = cs[:, 384:388]
        idxA = cs[:, 388:404]
        idxB = cs[:, 404:420]
        flags = cs[:, 420:422]
        c_eps = sm[:, 0:1]
        c_ln8 = sm[:, 1:2]
        lam = sm[:, 2:3]
        neglam = sm[:, 3:4]
        gsub = sm[:, 4:5]
        s12 = sm[:, 5:7]
        e12 = sm[:, 7:9]
        gmix = sm[:, 16:24]
        gffn = sm[:, 24:32]
        decb = sm[:, 32:40]
        lg = sm[:, 40:48]
        zf = sm[:, 48:52]
        zb = sm[:, 52:56]
        xif = sm[:, 56:60]
        xib = sm[:, 60:64]
        lgsf = sm[:, 64:66]
        lgsb = sm[:, 66:68]
        gfc = sm[:, 68:70]
        gbc = sm[:, 70:72]
        gfpow = sm[:, 72:104].rearrange("p (a b) -> p a b", a=2)
        gbpow = sm[:, 104:136].rearrange("p (a b) -> p a b", a=2)
        nrm = sm[:, 136:200].rearrange("p (a b) -> p a b", b=16)
        DT = rett[:, 0:512].rearrange("p (a b) -> p a b", a=2)
        dtmp = tFall[:, 0:512]
        lqs = tFall[:, 512:768]
        Sf = stS[:, 0:256]
        Sbp = [stS[:, 256:512], stS[:, 512:768]]

        state = {"stg": 0, "bank": 0, "nrm": 0}

        def mm(out, lhsT, rhs, start, stop, reads, writes, **kw):
            kb.add("pe", lambda e: e.matmul(out, lhsT, rhs, start=start, stop=stop, **kw), reads, writes)

        def tr(out, in_, reads, writes):
            kb.add("pe", lambda e: e.transpose(out, in_, identb[:, :]), list(reads) + ["identb"], writes)

        def act(out, in_, func, reads, writes, **kw):
            kb.add("act", lambda e: e.activation(out, in_, func, **kw), reads, writes)

        def ve(eng, name, reads, writes, *a, **kw):
            kb.add(eng, lambda e: getattr(e, name)(*a, **kw), reads, writes)

        def dma(out, in_, reads, writes, q="sp"):
            kb.add(q, lambda e: e.dma_start(out=out, in_=in_), reads, writes, dma=True)

        def psb(b):
            return ps[:, b, :]

        def psbf(b):
            return ps[:, b, :].bitcast(BF16)

        def nslot():
            state["nrm"] = (state["nrm"] + 1) % 4
            return state["nrm"]

        def load_w(dst, dkey, src, nch, ncols, fold):
            per = 2048 // nch
            c0 = 0
            part = 0
            while c0 < ncols:
                w = min(per, ncols - c0)
                s = state["stg"]
                state["stg"] = 1 - s
                sv = stg[s][:, 0:nch * w].rearrange("p (c n) -> p c n", c=nch)
                dma(sv, src[:, c0:c0 + w].rearrange("(c p) n -> p c n", p=128), [], [("stg", s)])
                if fold is None:
                    act(dst[:, :, c0:c0 + w], sv, AF.Identity, [("stg", s)], [dkey])
                else:
                    for c in range(nch):
                        fa = fold[:, c:c + 1] if fold.shape[1] > 1 else fold[:, 0:1]
                        act(dst[:, c, c0:c0 + w], sv[:, c, :], AF.Identity, [("stg", s), "consts"], [dkey], scale=fa)
                c0 += w
                part += 1

        def wkeys(dkey, nch, ncols, folded):
            return [dkey]

        def wslot(i, nch=8):
            return Rw[i][:, :].rearrange("p (c n) -> p c n", c=nch)

        def norm_tile(src_ap, srckey_reads, s, dst, dstkey, bank):
            k = nslot()
            ss = nrm[:, k, 0:1]
            rt = nrm[:, k, 1:2]
            rs = nrm[:, k, 2:3]
            act(junk[:, :], src_ap, AF.Square, srckey_reads, ["junk", ("nrm", k)], accum_out=ss)
            act(rt, ss, AF.Sqrt, [("nrm", k), "consts"], [("nrm", k, 1)], scale=1.0 / 1024, bias=c_eps)
            ve("dve", "reciprocal", [("nrm", k, 1)], [("nrm", k, 2)], rs, rt)
            act(xn[s][:, :], src_ap, AF.Identity, list(srckey_reads) + [("nrm", k, 2)], [("xn", s)], scale=rs)
            pb = psbf(bank)
            for c in range(8):
                tr(pb[:, c * 128:(c + 1) * 128], xn[s][:, c * 128:(c + 1) * 128], [("xn", s)], [("ps", bank)])
            ve("dve", "tensor_copy", [("ps", bank)], [dstkey], dst, pb.rearrange("p (c t) -> p c t", c=8))

        def rope(bank, ncols, tt, out_bf, outkey, tslot):
            G = ncols // 64
            X = ps[:, bank, 0:ncols].rearrange("p (g t d) -> p g t d", g=G, t=2)
            ia, ib = 2 * tslot, 2 * tslot + 1
            A = tF[ia][:, 0:ncols].rearrange("p (g t d) -> p g t d", g=G, t=2)
            B = tF[ib][:, 0:ncols].rearrange("p (g t d) -> p g t d", g=G, t=2)
            cosb = tabv[:, tt, 0:32].unsqueeze(1).unsqueeze(1).broadcast_to([128, G, 2, 32])
            sinb = tabv[:, tt, 32:64].unsqueeze(1).broadcast_to([128, G, 32])
            nsinb = tabv[:, tt, 64:96].unsqueeze(1).broadcast_to([128, G, 32])
            ka, kbk = ("tF", ia), ("tF", ib)
            act(tF[ia][:, 0:ncols], ps[:, bank, 0:ncols], AF.Identity, [("ps", bank)], [ka])
            ve("dve", "tensor_tensor", [ka, "tab"], [(kbk, 0)], B[:, :, 0, :], A[:, :, 1, :], nsinb, ALU.mult)
            ve("dve", "tensor_tensor", [ka, "tab"], [(kbk, 1)], B[:, :, 1, :], A[:, :, 0, :], sinb, ALU.mult)
            ve("dve", "tensor_tensor", [ka, "tab", (kbk, 0), (kbk, 1)], [ka], A, A, cosb, ALU.mult)
            ve(ROPE_ADD_ENG, "tensor_tensor", [ka, (kbk, 0), (kbk, 1)], [outkey], out_bf,
               tF[ia][:, 0:ncols], tF[ib][:, 0:ncols], ALU.add)

        def proj_tok(bank, ncols, ti, wv, wk):
            for c in range(8):
                mm(ps[:, bank, 0:ncols], hT[:, c, ti * 128:(ti + 1) * 128], wv[:, c, 0:ncols],
                   c == 0, c == 7, [("hT", ti)] + wk, [("ps", bank)])

        ve("dve", "memset", [], ["sm0"], sm[:, :], 0.0)
        kb.barrier()
        dma(cs[:, :], cst, [], ["cs"])
        dma(gmix, g_mix, [], ["gm"])
        dma(gffn, g_ffn, [], ["gf"])
        dma(gsub, g_sub, [], ["gs"])
        dma(decb, dec.partition_broadcast(128), [], ["decb"])
        dma(lqs, lqk.partition_broadcast(128), [], ["lq"])
        ve("dve", "memset", [], ["c0"], c_eps, EPS)
        ve("dve", "memset", [], ["c1"], c_ln8, LN8)
        ve("dve", "memset", [], ["zerob"], zerob[:, :], 0.0)
        ve("dve", "tensor_copy", ["cs"], ["identb"], identb[:, :], ident_f)
        ve("dve", "tensor_tensor", ["lq"], ["p1"], dtmp[:, 0:64], lqs[:, 0:64], lqs[:, 64:128], ALU.mult)
        ve("dve", "tensor_tensor", ["lq"], ["p2"], dtmp[:, 64:128], lqs[:, 128:192], lqs[:, 192:256], ALU.mult)
        act(dtmp[:, 128:192], dtmp[:, 0:64], AF.Identity, ["p1"], ["s1"], accum_out=s12[:, 0:1])
        act(dtmp[:, 192:256], dtmp[:, 64:128], AF.Identity, ["p2"], ["s2"], accum_out=s12[:, 1:2])
        act(e12, s12, AF.Exp, ["s1", "s2"], ["e12"])
        ve("dve", "tensor_tensor", ["e12"], ["lam0"], lam, e12[:, 0:1], e12[:, 1:2], ALU.subtract)
        ve("dve", "tensor_scalar", ["lam0"], ["neglam"], neglam, lam, -1.0, -0.2, ALU.mult, ALU.add)
        ve("dve", "tensor_scalar", ["gs"], ["gsub"], gsub, gsub, 0.8, 0.0, ALU.mult, ALU.add)
        act(tFall[:, 768:776], decb, AF.Exp, ["decb"], ["rt0"], scale=-1.0)
        ve("dve", "tensor_scalar", ["rt0"], ["rt1"], tFall[:, 776:784], tFall[:, 768:776], 1.0, 0.0, ALU.add, ALU.add)
        act(tFall[:, 784:792], tFall[:, 776:784], AF.Ln, ["rt1"], ["rt2"])
        ve("dve", "tensor_scalar", ["rt2"], ["lg"], lg, tFall[:, 784:792], -1.0, 0.0, ALU.mult, ALU.add)
        act(zf, lg[:, 0:4], AF.Exp, ["lg", "cs", "c1"], ["zf"], scale=jvec[:, 1:2], bias=c_ln8)
        act(zb, lg[:, 4:8], AF.Exp, ["lg", "cs", "c1"], ["zb"], scale=jvec[:, 0:1], bias=c_ln8)
        act(xif, lg[:, 0:4], AF.Exp, ["lg", "cs"], ["xif"], scale=jvec[:, 2:3])
        act(xib, lg[:, 4:8], AF.Exp, ["lg", "cs"], ["xib"], scale=jvec[:, 3:4])
        for pair in range(2):
            for hh in range(2):
                h = 2 * pair + hh
                ve("dve", "tensor_copy", ["lg"], [("lgs", pair, hh)], lgsf[hh * 64:(hh + 1) * 64, pair:pair + 1],
                   lg[hh * 64:(hh + 1) * 64, h:h + 1])
                ve("dve", "tensor_copy", ["lg"], [("lgsb", pair, hh)], lgsb[hh * 64:(hh + 1) * 64, pair:pair + 1],
                   lg[hh * 64:(hh + 1) * 64, 4 + h:5 + h])
        lgsk = [("lgs", p_, h_) for p_ in range(2) for h_ in range(2)] + [("lgsb", p_, h_) for p_ in range(2) for h_ in range(2)]
        act(gfc, lgsf, AF.Exp, lgsk, ["gfc"], scale=128.0)
        act(gbc, lgsb, AF.Exp, lgsk, ["gbc"], scale=128.0)
        for pair in range(2):
            act(gfpow[:, pair, :], idxA, AF.Exp, lgsk + ["cs"], [("gfpow", pair)], scale=lgsf[:, pair:pair + 1])
            act(gbpow[:, pair, :], idxB, AF.Exp, lgsk + ["cs"], [("gbpow", pair)], scale=lgsb[:, pair:pair + 1])
        for hh in range(2):
            for pair in range(2):
                h = 2 * pair + hh
                ve("dve", "tensor_scalar", ["lg", "cs"], ["dtA"], dtmp[:, 256:384], posm, lg[:, h:h + 1], 0.0, ALU.mult, ALU.add)
                ve("dve", "scalar_tensor_tensor", ["lg", "cs", "dtA"], ["dtB"], dtmp[:, 384:512], negm,
                   lg[:, 4 + h:5 + h], dtmp[:, 256:384], ALU.mult, ALU.add)
                act(DT[:, hh, pair * 128:(pair + 1) * 128], dtmp[:, 384:512], AF.Exp, ["dtB", "c1"], [("DT", h)], bias=c_ln8)
        ve("dve", "memset", [], ["Sf"], stS[:, :], 0.0)
        if stop_after:
            for nm_, t_ in [("RhT", RhT), ("RQT", RQT), ("RKT", RKT), ("RV1", RV1), ("RrnT", RrnT)]:
                ve("dve", "memset", [], [("dbgz", nm_)], t_[:, :], 0.0)
        kb.barrier()

        def phases():
            if stop_after == "setup":
                return
            dma(tabv, tabd[2048:4096, :].rearrange("(t p) c -> p t c", p=128), [], ["tab"])
            for t in range(16):
                s = t % 2
                dma(xt[s][:, :], xa[2048 + t * 128:2048 + (t + 1) * 128, :], [], [("xt", s)])
                norm_tile(xt[s][:, :], [("xt", s)], s, hT[:, :, t * 128:(t + 1) * 128], ("hT", t), 7)

            if stop_after == "normA":
                return

            def k_pass(slot, kname, tok_base, dstT):
                wv = wslot(slot)
                wk = wkeys(("wb", slot), 8, 512, True)
                for t in range(16):
                    b = t % 2
                    proj_tok(b, 512, t, wv, wk)
                    ob = tB[t % 3]
                    rope(b, 512, t, ob[:, 0:512], ("tB", t % 3), t % 2)
                    pb = psbf(4 + b)
                    for k in range(4):
                        tr(pb[:, k * 128:(k + 1) * 128], ob[:, k * 128:(k + 1) * 128], [("tB", t % 3)], [("ps", 4 + b)])
                    tok0 = tok_base + t * 128
                    ve("dve", "tensor_copy", [("ps", 4 + b)], [(kname, tok0)], dstT[:, :, tok0:tok0 + 128],
                       pb[:, 0:512].rearrange("p (h t) -> p h t", h=4))

            def v_pass(slot, tile_base):
                wv = wslot(slot)
                wk = wkeys(("wb", slot), 8, 512, True)
                for t in range(16):
                    b = 2 + t % 2
                    proj_tok(b, 512, t, wv, wk)
                    act(V1[:, tile_base + t, :, 0:128], ps[:, b, :].rearrange("p (h e) -> p h e", h=4), AF.Identity,
                        [("ps", b)], [("v1", tile_base + t)])

            load_w(wslot(0), ("wb", 0), w_in[:, 512:1024], 8, 512, gmix)
            load_w(wslot(1), ("wb", 1), w_in[:, 1024:1536], 8, 512, gmix)
            ve("dve", "memset", [], ["v1ones"], V1[:, :, :, 128:129], 1.0)
            if stop_after == "lw":
                return
            k_pass(0, "kt", 2048, KT)
            if stop_after == "kp":
                return
            v_pass(1, 16)
            if stop_after == "kvA":
                return
            load_w(wslot(0), ("wb", 0), w_in[:, 1792:2304], 8, 512, gmix)
            load_w(wslot(1), ("wb", 1), w_in[:, 2304:2560], 8, 256, gmix)
            wk0 = wkeys(("wb", 0), 8, 512, True)
            wk1 = wkeys(("wb", 1), 8, 256, True)

            def ret_kv(ro, rokey, rv_ap, rvkeys, ubank):
                kf = tB[2]
                ve("dve", "tensor_tensor", [rokey, "consts"], [("rkz", 0)], kf[:, 0:256].rearrange("p (h d) -> p h d", h=4),
                   ro.rearrange("p (h d) -> p h d", h=4), zf.unsqueeze(2).broadcast_to([128, 4, 64]), ALU.mult)
                ve("dve", "tensor_tensor", [rokey, "consts"], [("rkz", 1)], kf[:, 256:512].rearrange("p (h d) -> p h d", h=4),
                   ro.rearrange("p (h d) -> p h d", h=4), zb.unsqueeze(2).broadcast_to([128, 4, 64]), ALU.mult)
                for d in range(2):
                    for h in range(4):
                        pair, hh = h // 2, h % 2
                        mm(ps[hh * 64:(hh + 1) * 64, ubank, d * 256 + pair * 128:d * 256 + (pair + 1) * 128],
                           kf[:, d * 256 + h * 64:d * 256 + (h + 1) * 64], rv_ap[:, h * 128:(h + 1) * 128],
                           True, True, [("rkz", d)] + list(rvkeys), [("ps", ubank)])

            for t in range(16):
                ba, bb, bu = 0 + (t % 2), 2 + (t % 2), 6
                proj_tok(ba, 512, t, wslot(0), wk0)
                proj_tok(bb, 256, t, wslot(1), wk1)
                ro = tB[t % 2]
                rope(ba, 256, t, ro[:, 0:256], ("tB", t % 2), t % 2)
                rvo = xn[t % 2]
                act(rvo[:, 0:256], ps[:, ba, 256:512], AF.Identity, [("ps", ba)], [("xn", t % 2, 0)])
                act(rvo[:, 256:512], ps[:, bb, 0:256], AF.Identity, [("ps", bb)], [("xn", t % 2, 1)])
                ret_kv(ro[:, 0:256], ("tB", t % 2), rvo, [("xn", t % 2, 0), ("xn", t % 2, 1)], bu)
                for pair in range(2):
                    ve("dve", "scalar_tensor_tensor", [("ps", bu), "consts", "Sf"], ["Sf"],
                       Sf[:, pair * 128:(pair + 1) * 128], ps[:, bu, pair * 128:(pair + 1) * 128], gfpow[:, pair, t:t + 1],
                       Sf[:, pair * 128:(pair + 1) * 128], ALU.mult, ALU.add)
                    ve("dve", "scalar_tensor_tensor", [("ps", bu), "consts", ("Sb", 0)], [("Sb", 0)],
                       Sbp[0][:, pair * 128:(pair + 1) * 128], ps[:, bu, 256 + pair * 128:256 + (pair + 1) * 128],
                       gbpow[:, pair, t:t + 1], Sbp[0][:, pair * 128:(pair + 1) * 128], ALU.mult, ALU.add)
            ve("dve", "tensor_scalar", ["Sf", "cs"], ["Sf"], Sf, Sf, flags[:, 0:1], 0.0, ALU.mult, ALU.add)
            ve("dve", "tensor_scalar", [("Sb", 0), "cs"], [("Sb", 0)], Sbp[0], Sbp[0], flags[:, 1:2], 0.0, ALU.mult, ALU.add)
            kb.barrier()
            if stop_after == "other":
                return

            dma(tabv, tabd[0:2048, :].rearrange("(t p) c -> p t c", p=128), [], ["tab"])
            for t in range(16):
                s = t % 2
                dma(xt[s][:, :], xa[t * 128:(t + 1) * 128, :], [], [("xt", s)])
                norm_tile(xt[s][:, :], [("xt", s)], s, hT[:, :, t * 128:(t + 1) * 128], ("hT", t), 7)

            load_w(wslot(0), ("wb", 0), w_in[:, 1536:2048], 8, 512, gmix)
            load_w(wslot(1), ("wb", 1), w_in[:, 2048:2560], 8, 512, gmix)
            wk0 = wkeys(("wb", 0), 8, 512, True)
            wk1 = wkeys(("wb", 1), 8, 512, True)
            RQK = RQT[:, :].rearrange("p (k t) -> p k t", k=4)

            def rv_tile(i):
                return rvs[:, i // 4, (i % 4) * 512:(i % 4 + 1) * 512]

            for i in range(16):
                ba, bb, bu = 0 + (i % 2), 2 + (i % 2), 6
                proj_tok(ba, 512, i, wslot(0), wk0)
                proj_tok(bb, 512, i, wslot(1), wk1)
                ro = tB[i % 2]
                rope(ba, 512, i, ro[:, 0:512], ("tB", i % 2), i % 2)
                act(rv_tile(i), ps[:, bb, :], AF.Identity, [("ps", bb)], [("rv", i)])
                pb = psbf(4 + i % 2)
                for k in range(4):
                    tr(pb[:, k * 128:(k + 1) * 128], ro[:, k * 128:(k + 1) * 128], [("tB", i % 2)], [("ps", 4 + i % 2)])
                ve("dve", "tensor_copy", [("ps", 4 + i % 2)], [("rqk", i)], RQK[:, :, i * 128:(i + 1) * 128],
                   pb[:, 0:512].rearrange("p (k t) -> p k t", k=4))
                ret_kv(ro[:, 256:512], ("tB", i % 2), rv_tile(i), [("rv", i)], bu)
                act(Sfb[:, i, :], Sf, AF.Identity, ["Sf"], [("Sfb", i)])
                for pair in range(2):
                    ve("dve", "scalar_tensor_tensor", [("ps", bu), "Sf", "consts"], ["Sf"],
                       Sf[:, pair * 128:(pair + 1) * 128], Sf[:, pair * 128:(pair + 1) * 128], gfc[:, pair:pair + 1],
                       ps[:, bu, pair * 128:(pair + 1) * 128], ALU.mult, ALU.add)
                act(Ubs[:, i, :], ps[:, bu, 256:512], AF.Identity, [("ps", bu)], [("Ubs", i)])
            cur = 0
            for i in range(15, -1, -1):
                nxt = 1 - cur
                for pair in range(2):
                    ve("dve", "scalar_tensor_tensor", [("Ubs", i), ("Sb", cur), ("Sb", nxt), "consts"], [("Sb", nxt)],
                       Sbp[nxt][:, pair * 128:(pair + 1) * 128], Sbp[cur][:, pair * 128:(pair + 1) * 128], gbc[:, pair:pair + 1],
                       Ubs[:, i, pair * 128:(pair + 1) * 128], ALU.mult, ALU.add)
                act(Ubs[:, i, :], Sbp[cur], AF.Identity, [("Sb", cur), ("Sb", nxt)], [("Ubs", i)])
                cur = nxt
            for i in range(16):
                bS = [0, 1]
                bR = 2
                bC = [3, 4]
                tsl = slice(i * 128, (i + 1) * 128)
                for pair in range(2):
                    for hh in range(2):
                        rows = slice(hh * 64, (hh + 1) * 64)
                        mm(ps[:, bS[hh], pair * 128:(pair + 1) * 128], RQK[rows, 2 + pair, tsl], RQK[rows, pair, tsl],
                           True, True, [("rqk", i)], [("ps", bS[hh])])
                PT = tB[i % 2][:, 0:512].rearrange("p (a b) -> p a b", a=2)
                for hh in range(2):
                    ve("dve", "tensor_tensor", [("ps", bS[hh]), "consts"], [("PT", i % 2, hh)], PT[:, hh, :], ps[:, bS[hh], 0:256],
                       DT[:, hh, :], ALU.mult)
                for h in range(4):
                    pair, hh = h // 2, h % 2
                    mm(ps[:, bR, h * 128:(h + 1) * 128], PT[:, hh, pair * 128:(pair + 1) * 128], rv_tile(i)[:, h * 128:(h + 1) * 128],
                       True, True, [("PT", i % 2, hh), ("rv", i)], [("ps", bR)])
                for d in range(2):
                    St = Sfb if d == 0 else Ubs
                    for h in range(4):
                        pair, hh = h // 2, h % 2
                        rows = slice(hh * 64, (hh + 1) * 64)
                        mm(ps[:, bC[hh], d * 256 + pair * 128:d * 256 + (pair + 1) * 128], RQK[rows, pair, tsl],
                           St[rows, i, pair * 128:(pair + 1) * 128], True, True,
                           [("rqk", i), ("Sfb", i), ("Ubs", i)], [("ps", bC[hh])])
                rt_ = tF[i % 2]
                act(rt_[:, :], ps[:, bR, :], AF.Identity, [("ps", bR)], [("rt", i % 2)])
                k = nslot()
                for h in range(4):
                    pair, hh = h // 2, h % 2
                    ve("dve", "scalar_tensor_tensor", [("ps", bC[hh]), ("rt", i % 2), "consts"], [("rt", i % 2, h)],
                       rt_[:, h * 128:(h + 1) * 128], ps[:, bC[hh], pair * 128:(pair + 1) * 128], xif[:, h:h + 1],
                       rt_[:, h * 128:(h + 1) * 128], ALU.mult, ALU.add)
                    ve("dve", "scalar_tensor_tensor", [("ps", bC[hh]), ("rt", i % 2, h), "consts"], [("rt", i % 2, h)],
                       rt_[:, h * 128:(h + 1) * 128], ps[:, bC[hh], 256 + pair * 128:256 + (pair + 1) * 128], xib[:, h:h + 1],
                       rt_[:, h * 128:(h + 1) * 128], ALU.mult, ALU.add)
                    act(junk[:, 0:128], rt_[:, h * 128:(h + 1) * 128], AF.Square, [("rt", i % 2, h)], ["junk", ("nrm", k, "s", h)],
                        accum_out=nrm[:, k, h:h + 1])
                act(nrm[:, k, 4:8], nrm[:, k, 0:4], AF.Sqrt, [("nrm", k, "s", h_) for h_ in range(4)] + ["consts"], [("nrm", k, 1)],
                    scale=1.0 / 128, bias=c_eps)
                ve("dve", "reciprocal", [("nrm", k, 1)], [("nrm", k, 2)], nrm[:, k, 8:12], nrm[:, k, 4:8])
                rnb = xn[i % 2]
                for h in range(4):
                    ve("dve", "tensor_scalar", [("rt", i % 2, h), ("nrm", k, 2)], [("rnb", i % 2, h)], rnb[:, h * 128:(h + 1) * 128],
                       rt_[:, h * 128:(h + 1) * 128], nrm[:, k, 8 + h:9 + h], 0.0, ALU.mult, ALU.add)
                pb = psbf(5)
                for h in range(4):
                    tr(pb[:, h * 128:(h + 1) * 128], rnb[:, h * 128:(h + 1) * 128], [("rnb", i % 2, h)], [("ps", 5)])
                ve("dve", "tensor_copy", [("ps", 5)], [("rnT", i)], rnT[:, :, tsl], pb[:, 0:512].rearrange("p (h t) -> p h t", h=4))
            kb.barrier()
            if stop_after == "ret":
                return

            load_w(wslot(0), ("wb", 0), w_in[:, 0:512], 8, 512, gmix)
            load_w(wslot(1), ("wb", 1), w_in[:, 512:1024], 8, 512, gmix)
            ve("dve", "memset", [], ["v1ones"], V1[:, 0:16, :, 128:129], 1.0)
            k_pass(0, "qt", 0, QT)
            k_pass(1, "kt", 0, KT)
            load_w(wslot(0), ("wb", 0), w_in[:, 1024:1536], 8, 512, gmix)
            v_pass(0, 0)
            kb.barrier()
            if stop_after == "own1":
                return

            def acc_ap(slot):
                b, c0 = slot // 3, (slot % 3) * 129
                return ps[:, b, c0:c0 + 129], b

            for h in range(4):
                for qb in range(4):
                    for b in range(3):
                        mm(ps[:, b, :], zerob[:, 0:128], zerob[:, :], True, True, ["zerob"], [("ps", b)])
                    qs = slice(qb * 512, (qb + 1) * 512)
                    for kt in range(32):
                        par = kt % 2
                        ks = slice(kt * 128, (kt + 1) * 128)
                        kkey = ("kt", kt * 128)
                        qkeys = [("qt", qb * 512 + j * 128) for j in range(4)]
                        mm(ps[:, 3 + 2 * par, :], KT[0:64, h, ks], QT[0:64, h, qs], True, True, [kkey] + qkeys, [("ps", 3 + 2 * par)])
                        mm(ps[:, 4 + 2 * par, :], KT[64:128, h, ks], QT[64:128, h, qs], True, True, [kkey] + qkeys, [("ps", 4 + 2 * par)])
                        E = tB[kt % 3]
                        act(E[:, :].rearrange("p (c q) -> p c q", c=2), ps[:, 3 + 2 * par:5 + 2 * par, :], AF.Exp,
                            [("ps", 3 + 2 * par), ("ps", 4 + 2 * par)], [("tB", kt % 3)], scale=0.125)
                        for c in range(2):
                            for j in range(4):
                                a, b = acc_ap(c * 4 + j)
                                mm(a, E[:, c * 512 + j * 128:c * 512 + (j + 1) * 128], V1[:, kt, h, :], False, kt == 31,
                                   [("tB", kt % 3), ("v1", kt), "v1ones"], [("ps", b)], skip_group_check=True)
                    k = nslot()
                    rec = nrm[:, k, 0:8]
                    for b in range(3):
                        n = 3 if b < 2 else 2
                        ve("dve", "reciprocal", [("ps", b)], [("nrm", k, "r", b)], rec[:, 3 * b:3 * b + n],
                           ps[:, b, 0:129 * n].rearrange("p (s c) -> p s c", c=129)[:, :, 128])
                    rk_ = [("nrm", k, "r", b) for b in range(3)]
                    ve("dve", "tensor_scalar", rk_ + ["consts"], [("nrm", k, "nl")], nrm[:, k, 8:12], rec[:, 4:8], neglam, 0.0, ALU.mult, ALU.add)
                    af = [tF[0], tF[1]]
                    for j in range(4):
                        a0, b0 = acc_ap(j)
                        a1, b1 = acc_ap(4 + j)
                        dst = af[j // 2][:, (j % 2) * 128:(j % 2 + 1) * 128]
                        ve("dve", "tensor_scalar", [("ps", b0)] + rk_, [("af", j)], dst, a0[:, 0:128], rec[:, j:j + 1], 0.0, ALU.mult, ALU.add)
                        ve("dve", "scalar_tensor_tensor", [("ps", b1), ("af", j), ("nrm", k, "nl")], [("af", j)], dst, a1[:, 0:128],
                           nrm[:, k, 8 + j:9 + j], dst, ALU.mult, ALU.add)
                    for jj in range(2):
                        ve("dve", "tensor_tensor", [("af", 2 * jj), ("af", 2 * jj + 1)], [("afs", jj)], af[jj][:, 256:512], af[jj][:, 0:256],
                           af[jj][:, 0:256], ALU.mult)
                        ve("dve", "tensor_reduce", [("afs", jj)], [("nrm", k, "ss", jj)], nrm[:, k, 12 + 2 * jj:14 + 2 * jj],
                           af[jj][:, 256:512].rearrange("p (j e) -> p j e", j=2), mybir.AxisListType.X, ALU.add)
                    k2 = nslot()
                    act(nrm[:, k2, 0:4], nrm[:, k, 12:16], AF.Sqrt, [("nrm", k, "ss", j) for j in range(2)] + ["consts"], [("nrm", k2, 1)],
                        scale=1.0 / 128, bias=c_eps)
                    ve("dve", "reciprocal", [("nrm", k2, 1)], [("nrm", k2, 2)], nrm[:, k2, 4:8], nrm[:, k2, 0:4])
                    anb = xn[(h * 4 + qb) % 2]
                    for j in range(4):
                        dst = af[j // 2][:, (j % 2) * 128:(j % 2 + 1) * 128]
                        ve("dve", "tensor_scalar", [("af", j), ("nrm", k2, 2)], [("anb", j)], anb[:, j * 128:(j + 1) * 128], dst,
                           nrm[:, k2, 4 + j:5 + j], 0.0, ALU.mult, ALU.add)
                    pb = psbf(7)
                    for j in range(4):
                        tr(pb[:, j * 128:(j + 1) * 128], anb[:, j * 128:(j + 1) * 128], [("anb", j)], [("ps", 7)])
                    ve("dve", "tensor_copy", [("ps", 7)], [("anT", h, qb)],
                       QT[:, h, qs], pb[:, 0:512])
            kb.barrier()
            if stop_after == "att":
                return

            load_w(wslot(0), ("wb", 0), w_in[:, 2560:3072], 8, 512, gmix)
            wk0 = wkeys(("wb", 0), 8, 512, True)
            n = 0
            for h in range(4):
                for tb in range(4):
                    b = n % 4
                    for c in range(8):
                        mm(ps[:, b, :], wslot(0)[:, c, h * 128:(h + 1) * 128], hT[:, c, tb * 512:(tb + 1) * 512], c == 0, c == 7,
                           wk0 + [("hT", tb * 4 + j) for j in range(4)], [("ps", b)])
                    sg = tB[n % 3]
                    act(sg[:, 0:512], ps[:, b, :], AF.Silu, [("ps", b)], [("tB", n % 3)])
                    ve("dve", "tensor_tensor", [("tB", n % 3)], [("rnTg", h, tb)], rnT[:, h, tb * 512:(tb + 1) * 512],
                       rnT[:, h, tb * 512:(tb + 1) * 512], sg[:, 0:512], ALU.mult)
                    n += 1
            for sub in range(2):
                for g2 in range(2):
                    gcol = (3072 if sub == 0 else 4096) + g2 * 512
                    load_w(wslot(0), ("wb", 0), w_in[:, gcol:gcol + 512], 8, 512, gmix)
                    wup = w_upa if sub == 0 else w_upb
                    load_w(wslot(1, 4)[:, :, 0:512], ("wb", 1), wup[:, g2 * 512:(g2 + 1) * 512], 4, 512, gsub if sub == 0 else None)
                    wk0 = wkeys(("wb", 0), 8, 512, True)
                    wk1 = wkeys(("wb", 1), 4, 512, sub == 0)
                    srcT = QT if sub == 0 else rnT
                    for ti in range(16):
                        b1, b2 = (ti % 2) * 2, (ti % 2) * 2 + 1
                        tsl = slice(ti * 128, (ti + 1) * 128)
                        proj_tok(b1, 512, ti, wslot(0), wk0)
                        for h in range(4):
                            mm(ps[:, b2, :], srcT[:, h, tsl], wslot(1, 4)[:, h, 0:512], h == 0, h == 3,
                               wk1 + ([("rnTg", h, ti // 4)] if sub == 1 else []), [("ps", b2)])
                        sgm = tF[ti % 2]
                        act(sgm[:, :], ps[:, b1, :], AF.Sigmoid, [("ps", b1)], [("sgm", ti % 2)])
                        if sub == 0:
                            ve("dve", "tensor_tensor", [("sgm", ti % 2), ("ps", b2)], [("mtok", ti, g2)],
                               mtok[:, ti, g2 * 512:(g2 + 1) * 512], sgm[:, :], ps[:, b2, :], ALU.mult)
                        else:
                            ve("dve", "tensor_tensor", [("sgm", ti % 2), ("ps", b2)], [("sgm", ti % 2)], sgm[:, :], sgm[:, :], ps[:, b2, :], ALU.mult)
                            mf = tB[ti % 2]
                            ve("dve", "tensor_tensor", [("sgm", ti % 2), ("mtok", ti, g2)], [("tB", ti % 2)], mf[:, 0:512], sgm[:, :],
                               mtok[:, ti, g2 * 512:(g2 + 1) * 512], ALU.add)
                            pb = psbf(4 + ti % 2)
                            for k in range(4):
                                tr(pb[:, k * 128:(k + 1) * 128], mf[:, k * 128:(k + 1) * 128], [("tB", ti % 2)], [("ps", 4 + ti % 2)])
                            ve("dve", "tensor_copy", [("ps", 4 + ti % 2)], [("mT", ti, g2)], mT[:, g2 * 4:(g2 + 1) * 4, tsl],
                               pb[:, 0:512].rearrange("p (c t) -> p c t", c=4))
            kb.barrier()
            if stop_after == "d1":
                return

            load_w(wslot(0), ("wb", 0), w_o[:, 0:512], 8, 512, None)
            load_w(wslot(1), ("wb", 1), w_o[:, 512:1024], 8, 512, None)
            wko = [wkeys(("wb", 0), 8, 512, False), wkeys(("wb", 1), 8, 512, False)]
            for ti in range(16):
                s = ti % 2
                tsl = slice(ti * 128, (ti + 1) * 128)
                dma(xt[s][:, :], xa[ti * 128:(ti + 1) * 128, :], [], [("xt", s)])
                for half in range(2):
                    b = (ti % 2) * 2 + half
                    for c in range(8):
                        mm(ps[:, b, :], mT[:, c, tsl], wslot(half)[:, c, :], c == 0, c == 7,
                           wko[half] + [("mT", ti, 0), ("mT", ti, 1)], [("ps", b)])
                    ve("dve", "tensor_tensor", [("ps", b), ("xt", s)], [("x1", ti, half)], x1(ti)[:, half * 512:(half + 1) * 512],
                       xt[s][:, half * 512:(half + 1) * 512], ps[:, b, :], ALU.add)
                norm_tile(x1(ti), [("x1", ti, 0), ("x1", ti, 1)], s, mT[:, :, tsl], ("mT", ti, 0), 6 + ti % 2)
                kb.last_w[("mT", ti, 1)] = kb.last_w[("mT", ti, 0)]
            kb.barrier()
            if stop_after == "d2":
                return

            dma(gfb, g_fin.partition_broadcast(128), [], ["gfb"])
            slotsA = [Rw[0][:, :], Rw[1][:, :], RQT[:, 0:4096]]
            slotsB = [RQT[:, 4096:8192], RrnT[:, 0:4096], RrnT[:, 4096:8192]]
            nd = 0
            for fg in range(6):
                ncol = 512 if fg < 5 else 256
                nch = ncol // 128
                sl = slotsA if fg % 2 == 0 else slotsB
                sid = fg % 2
                Wg = sl[0][:, 0:4096].rearrange("p (c n) -> p c n", c=8)
                Wu = sl[1][:, 0:4096].rearrange("p (c n) -> p c n", c=8)
                Wd = sl[2][:, 0:4096].rearrange("p (c n) -> p c n", c=4)
                load_w(Wg[:, :, 0:ncol], ("fw", sid, 0), w_fg[:, fg * 512:fg * 512 + ncol], 8, ncol, gffn)
                load_w(Wu[:, :, 0:ncol], ("fw", sid, 1), w_fu[:, fg * 512:fg * 512 + ncol], 8, ncol, gffn)
                load_w(Wd[:, 0:nch, :], ("fw", sid, 2), w_fd[fg * 512:fg * 512 + ncol, :], nch, 1024, None)
                kg = wkeys(("fw", sid, 0), 8, ncol, True)
                ku = wkeys(("fw", sid, 1), 8, ncol, True)
                kd = wkeys(("fw", sid, 2), nch, 1024, False)
                for tb in range(4):
                    actT = (xn[0] if tb % 2 == 0 else xn[1])
                    actT2 = (tB[0] if tb % 2 == 0 else tB[1])
                    hk = [("mT", tb * 4 + j, 0) for j in range(4)]
                    for ch in range(nch):
                        bg, bu = (ch % 2) * 2, (ch % 2) * 2 + 1
                        for c in range(8):
                            mm(ps[:, bg, :], Wg[:, c, ch * 128:(ch + 1) * 128], mT[:, c, tb * 512:(tb + 1) * 512], c == 0, c == 7,
                               kg + hk, [("ps", bg)])
                        for c in range(8):
                            mm(ps[:, bu, :], Wu[:, c, ch * 128:(ch + 1) * 128], mT[:, c, tb * 512:(tb + 1) * 512], c == 0, c == 7,
                               ku + hk, [("ps", bu)])
                        sgf = tF[ch % 2]
                        act(sgf[:, :], ps[:, bg, :], AF.Silu, [("ps", bg)], [("sgf", ch % 2)])
                        dstA = (actT if ch < 2 else actT2)[:, (ch % 2) * 512:(ch % 2 + 1) * 512]
                        ve("dve", "tensor_tensor", [("sgf", ch % 2), ("ps", bu)], [("actT", tb % 2, ch)], dstA, sgf[:, :], ps[:, bu, :], ALU.mult)
                    for tt in range(4):
                        ti = tb * 4 + tt
                        for half in range(2):
                            bd = 4 + nd % 4
                            nd += 1
                            for ch in range(nch):
                                srcA = (actT if ch < 2 else actT2)[:, (ch % 2) * 512 + tt * 128:(ch % 2) * 512 + (tt + 1) * 128]
                                mm(ps[:, bd, :], srcA, Wd[:, ch, half * 512:(half + 1) * 512], ch == 0, ch == nch - 1,
                                   kd + [("actT", tb % 2, ch)], [("ps", bd)])
                            ve("dve", "tensor_tensor", [("ps", bd)], [("x1", ti, half)], x1(ti)[:, half * 512:(half + 1) * 512],
                               x1(ti)[:, half * 512:(half + 1) * 512], ps[:, bd, :], ALU.add)
            for ti in range(16):
                s = ti % 2
                k = nslot()
                ss, rt, rs = nrm[:, k, 0:1], nrm[:, k, 1:2], nrm[:, k, 2:3]
                act(junk[:, :], x1(ti), AF.Square, [("x1", ti, 0), ("x1", ti, 1)], ["junk", ("nrm", k)], accum_out=ss)
                act(rt, ss, AF.Sqrt, [("nrm", k), "consts"], [("nrm", k, 1)], scale=1.0 / 1024, bias=c_eps)
                ve("dve", "reciprocal", [("nrm", k, 1)], [("nrm", k, 2)], rs, rt)
                ve("dve", "scalar_tensor_tensor", [("x1", ti, 0), ("x1", ti, 1), ("nrm", k, 2), "gfb"], [("xt", s)], xt[s][:, :], x1(ti), rs,
                   gfb, ALU.mult, ALU.mult)
                dma(y[ti * 128:(ti + 1) * 128, :], xt[s][:, :], [("xt", s)], [("yout", ti % 4)])
            kb.add("sp", lambda e: e.nop(), [("yout", i) for i in range(4)], [])


        phases()
        if stop_after:
            kb.barrier()
            names = []
            lst = [("RhT", RhT[:, :]), ("RQT", RQT[:, :]), ("RKT", RKT[:, :]), ("RV1", RV1[:, :]), ("RrnT", RrnT[:, :]),
                   ("stS", stS[:, :]), ("sm", sm[:, :]), ("rett", rett[:, :])]
            if stop_after == "d2":
                lst = [("x1a", RhT[:, :].bitcast(F32)), ("x1b", RKT[:, :].bitcast(F32)), ("RV1", RV1[:, :])]
            for name, t in lst:
                d = nc.dram_tensor("dbg_" + name, list(t.shape), t.dtype, kind="ExternalOutput").ap()
                dma(d, t, [], [("dbg", name)])
                names.append(("dbg", name))
            kb.add("sp", lambda e: e.nop(), names, [])
        kb.emit(es)
    return nc


_NC_CACHE = {}


def _host_consts():
    c = np.zeros((128, 424), np.float32)
    c[:, 0:128] = np.eye(128, dtype=np.float32)
    j = np.arange(128, dtype=np.float32)
    diff = j[None, :] - j[:, None]
    c[:, 128:256] = np.maximum(diff, 0)
    c[:, 256:384] = np.maximum(-diff, 0)
    c[:, 384] = j
    c[:, 385] = 127 - j
    c[:, 386] = j + 1
    c[:, 387] = 128 - j
    jj = np.arange(16, dtype=np.float32)
    c[:, 388:404] = (128.0 * (15 - jj))[None, :]
    c[:, 404:420] = (128.0 * jj)[None, :]
    return c


def _rope_table():
    inv_freq = (10000.0 ** (-np.arange(0, 64, 2, dtype=np.float32) / np.float32(64))).astype(np.float32)
    ang = np.arange(4096, dtype=np.float32)[:, None] * inv_freq[None, :]
    cos = np.cos(ang).astype(np.float32)
    sin = np.sin(ang).astype(np.float32)
    return np.concatenate([cos, sin, -sin], axis=1).astype(np.float32)


def make_in_maps(inputs, small=False):
    f = lambda a: np.ascontiguousarray(np.asarray(a, dtype=np.float32))
    x = f(inputs["x"])
    tabfull = _rope_table()
    cbase = _host_consts()
    shared = {
        "w_in": f(inputs["w_in"][0]), "w_up_diff": f(inputs["w_up_diff"][0]), "w_up_ret": f(inputs["w_up_ret"][0]),
        "w_o": f(inputs["w_o"][0]), "w_ffn_gate": f(inputs["w_ffn_gate"][0]), "w_ffn_up": f(inputs["w_ffn_up"][0]),
        "w_ffn_down": f(inputs["w_ffn_down"][0]), "g_mix": f(np.asarray(inputs["g_mix"][0]).reshape(8, 128).T), "g_ffn": f(np.asarray(inputs["g_ffn"][0]).reshape(8, 128).T),
        "g_final": f(inputs["g_final"]), "diff_subln_g": f(np.asarray(inputs["diff_subln_g"][0]).reshape(128, 1)),
        "lqk": f(np.concatenate([inputs["diff_lq1"][0], inputs["diff_lk1"][0], inputs["diff_lq2"][0], inputs["diff_lk2"][0]], 0)),
        "dec": f(np.concatenate([inputs["ret_decay_fwd"][0], inputs["ret_decay_bwd"][0]], 0)),
    }
    if small:
        for k_ in ("w_ffn_gate", "w_ffn_up", "w_ffn_down"):
            shared[k_] = np.zeros((1, 1), np.float32)
    maps = []
    for core in range(8):
        b, half = core // 2, core % 2
        own = slice(half * 2048, (half + 1) * 2048)
        oth = slice((1 - half) * 2048, (2 - half) * 2048)
        xa = np.ascontiguousarray(np.concatenate([x[b, own], x[b, oth]], 0))
        tab = np.ascontiguousarray(np.concatenate([tabfull[own], tabfull[oth]], 0))
        c = cbase.copy()
        c[:, 420] = 1.0 if half == 1 else 0.0
        c[:, 421] = 1.0 if half == 0 else 0.0
        m = dict(shared)
        m.update({"xa": xa, "tab": tab, "cst": c})
        maps.append(m)
    return maps


def kernel(**inputs):
    if "nc" not in _NC_CACHE:
        _NC_CACHE["nc"] = build()
    nc = _NC_CACHE["nc"]
    maps = make_in_maps(inputs)
    res = run_bass_kernel_spmd(nc, maps, core_ids=list(range(8)))
    out = np.zeros((4, 4096, 1024), np.float32)
    for core in range(8):
        b, half = core // 2, core % 2
        out[b, half * 2048:(half + 1) * 2048] = np.asarray(res.results[core]["y"], dtype=np.float32)
    return out
```
